# Optimizing a Trainium2 kernel written in Bass

```python
import jax
import jax.numpy as jnp
from jax import lax
import numpy as np

D_MODEL = 1024
BATCH = 32
SEQ = 256
DEPTH = 2
DEC_BATCH = 2
DEC_SEQ = 4096
PAST_LEN = 256

GRID_W = 64
D_LRU = 1024
LRU_BLOCKS = 8
LRU_BLOCK = D_LRU // LRU_BLOCKS
LRU_CONV_W = 4
LRU_CONV_PAD_L = 2
LRU_C = 8.0
HEAD_DIM = 64
HEADS_B = 8
KV_HEADS_B = 2
HEADS_C = 8
KV_HEADS_C = 2
D_QB = HEADS_B * HEAD_DIM
D_KVB = KV_HEADS_B * HEAD_DIM
D_QC = HEADS_C * HEAD_DIM
D_KVC = KV_HEADS_C * HEAD_DIM
WINDOW = 128
Q_BLOCK = 128
D_FF = 2816
FFN_CONV_W = 3
FFN_CONV_PAD_L = 1
ROPE_THETA = 10000.0
NORM_EPS = 1e-6
NEG_INF = -1e30
IN_SIZES = (D_LRU, D_LRU, D_QB, D_KVB, D_KVB, D_QC, D_KVC, D_KVC, D_MODEL, D_MODEL, D_MODEL)
D_IN = sum(IN_SIZES)

kernel_name = "hybrid_diffusion_prefix_step"


def rmsnorm(x, g):
    xf = x.astype(jnp.float32)
    y = xf * lax.rsqrt(jnp.mean(xf * xf, axis=-1, keepdims=True) + NORM_EPS)
    return (y * g.astype(jnp.float32)).astype(x.dtype)


def adaln_params(cond, w_mod, b_mod):
    m = jax.nn.silu(cond) @ w_mod + b_mod
    return [t[:, None, :] for t in jnp.split(m, 6, axis=-1)]


def modulate(x, g, shift, scale):
    return rmsnorm(x, g) * (1 + scale) + shift


def split_in(z):
    idx = np.cumsum(np.array(IN_SIZES))[:-1].tolist()
    return jnp.split(z, idx, axis=-1)


def dwconv(x, w, b, pad_left):
    k_w = w.shape[0]
    s = x.shape[1]
    xp = jnp.pad(x, ((0, 0), (pad_left, k_w - 1 - pad_left), (0, 0)))
    y = xp[:, 0:s] * w[0] + b
    for k in range(1, k_w):
        y = y + xp[:, k:k + s] * w[k]
    return y


def _rotate(x, pos):
    nf = x.shape[-1] // 2
    inv = ROPE_THETA ** (-jnp.arange(nf, dtype=jnp.float32) / nf)
    ang = pos.astype(jnp.float32)[:, None] * inv[None, :]
    cos = jnp.cos(ang)[None, :, None, :]
    sin = jnp.sin(ang)[None, :, None, :]
    x1, x2 = x[..., :nf], x[..., nf:]
    return jnp.concatenate([x1 * cos - x2 * sin, x1 * sin + x2 * cos], axis=-1)


def axial_rope(x):
    s = x.shape[1]
    rows = s // GRID_W
    row, col = jnp.meshgrid(jnp.arange(rows), jnp.arange(GRID_W), indexing="ij")
    xf = x.astype(jnp.float32)
    half = x.shape[-1] // 2
    y = jnp.concatenate([_rotate(xf[..., :half], row.reshape(-1)),
                         _rotate(xf[..., half:], col.reshape(-1))], axis=-1)
    return y.astype(x.dtype)


def rglru(x, wa, ba, wx, bx, lam, h0, reverse):
    b, s, c = x.shape
    xb = x.reshape(b, s, LRU_BLOCKS, LRU_BLOCK)
    r = jax.nn.sigmoid(jnp.einsum("bsnc,ncd->bsnd", xb, wa).reshape(b, s, c) + ba)
    i = jax.nn.sigmoid(jnp.einsum("bsnc,ncd->bsnd", xb, wx).reshape(b, s, c) + bx)
    log_a = -LRU_C * r.astype(jnp.float32) * jax.nn.softplus(-lam.astype(jnp.float32))
    a = jnp.exp(log_a)
    u = jnp.sqrt(-jnp.expm1(2.0 * log_a)) * (i * x).astype(jnp.float32)

    def step(h, au):
        a_t, u_t = au
        h = a_t * h + u_t
        return h, h

    h_last, hs = lax.scan(step, h0.astype(jnp.float32),
                          (a.transpose(1, 0, 2), u.transpose(1, 0, 2)), reverse=reverse)
    return hs.transpose(1, 0, 2).astype(x.dtype), h_last


def gqa_blocked(q, k, v, sink=None):
    b, s, hq, hd = q.shape
    hkv = k.shape[2]
    g = hq // hkv
    nb = s // Q_BLOCK
    qb = q.reshape(b, nb, Q_BLOCK, hkv, g, hd).transpose(1, 0, 2, 3, 4, 5)
    scale = hd ** -0.5

    def block(qblk):
        sc = jnp.einsum("bqkgd,blkd->bkgql", qblk, k,
                        preferred_element_type=jnp.float32) * scale
        if sink is None:
            p = jax.nn.softmax(sc, axis=-1)
        else:
            sk = jnp.broadcast_to(sink.astype(jnp.float32).reshape(1, hkv, g, 1, 1),
                                  sc.shape[:-1] + (1,))
            p = jax.nn.softmax(jnp.concatenate([sc, sk], axis=-1), axis=-1)[..., :-1]
        return jnp.einsum("bkgql,blkd->bqkgd", p.astype(v.dtype), v)

    o = lax.map(block, qb)
    return o.transpose(1, 0, 2, 3, 4, 5).reshape(b, s, hq * hd)


def window_attn_latent(q, k, v, k_ctx, v_ctx, sink):
    b, s, hq, hd = q.shape
    hkv = k.shape[2]
    g = hq // hkv
    nb = s // WINDOW
    pad = ((0, 0), (WINDOW, WINDOW), (0, 0), (0, 0))
    kp = jnp.pad(k, pad).reshape(b, nb + 2, WINDOW, hkv, hd)
    vp = jnp.pad(v, pad).reshape(b, nb + 2, WINDOW, hkv, hd)
    k_band = jnp.concatenate([kp[:, :-2], kp[:, 1:-1], kp[:, 2:]], axis=2)
    v_band = jnp.concatenate([vp[:, :-2], vp[:, 1:-1], vp[:, 2:]], axis=2)
    qb = q.reshape(b, nb, WINDOW, hkv, g, hd)
    scale = hd ** -0.5
    s_lat = jnp.einsum("bnqkgd,bnlkd->bnkgql", qb, k_band,
                       preferred_element_type=jnp.float32) * scale
    blk = jnp.arange(nb)[:, None, None] * WINDOW
    q_abs = blk + jnp.arange(WINDOW)[None, :, None]
    k_abs = blk + jnp.arange(3 * WINDOW)[None, None, :] - WINDOW
    valid = (jnp.abs(q_abs - k_abs) <= WINDOW) & (k_abs >= 0) & (k_abs < s)
    s_lat = jnp.where(valid[None, :, None, None], s_lat, NEG_INF)
    s_ctx = jnp.einsum("bnqkgd,blkd->bnkgql", qb, k_ctx,
                       preferred_element_type=jnp.float32) * scale
    sk = jnp.broadcast_to(sink.astype(jnp.float32).reshape(1, 1, hkv, g, 1, 1),
                          s_ctx.shape[:-1] + (1,))
    p = jax.nn.softmax(jnp.concatenate([s_lat, s_ctx, sk], axis=-1), axis=-1)
    p_lat = p[..., :3 * WINDOW].astype(v.dtype)
    p_ctx = p[..., 3 * WINDOW:-1].astype(v.dtype)
    o = (jnp.einsum("bnkgql,bnlkd->bnqkgd", p_lat, v_band)
         + jnp.einsum("bnkgql,blkd->bnqkgd", p_ctx, v_ctx))
    return o.reshape(b, s, hq * hd)


def token_mixers(h, lp, ctx):
    b, s, _ = h.shape
    xa, ya, qb, kb, vb, qc, kc, vc, ga, gb, gc = split_in(h @ lp["w_in"])
    is_ctx = ctx is None
    xconv = dwconv(xa, lp["lru_conv_w"], lp["lru_conv_b"], LRU_CONV_PAD_L)
    h0 = jnp.zeros((b, 2, D_LRU), jnp.float32) if is_ctx else ctx["state"]
    h_f, last_f = rglru(xconv, lp["lru_wa"][0], lp["lru_ba"][0], lp["lru_wx"][0],
                        lp["lru_bx"][0], lp["lru_lam"][0], h0[:, 0], reverse=False)
    h_b, last_b = rglru(xconv, lp["lru_wa"][1], lp["lru_ba"][1], lp["lru_wx"][1],
                        lp["lru_bx"][1], lp["lru_lam"][1], h0[:, 1], reverse=True)
    o_a = (h_f + h_b) * jax.nn.gelu(ya)
    qb = rmsnorm(qb.reshape(b, s, HEADS_B, HEAD_DIM), lp["qnorm_g"])
    kb = rmsnorm(kb.reshape(b, s, KV_HEADS_B, HEAD_DIM), lp["knorm_g"])
    vb = vb.reshape(b, s, KV_HEADS_B, HEAD_DIM)
    qc = qc.reshape(b, s, HEADS_C, HEAD_DIM)
    kc = kc.reshape(b, s, KV_HEADS_C, HEAD_DIM)
    vc = vc.reshape(b, s, KV_HEADS_C, HEAD_DIM)
    if is_ctx:
        o_b = gqa_blocked(qb, kb, vb)
        o_c = gqa_blocked(qc, kc, vc, lp["sink_c"])
    else:
        o_b = gqa_blocked(axial_rope(qb),
                          jnp.concatenate([ctx["kb"], axial_rope(kb)], axis=1),
                          jnp.concatenate([ctx["vb"], vb], axis=1))
        o_c = window_attn_latent(axial_rope(qc), axial_rope(kc), vc,
                                 ctx["kc"], ctx["vc"], lp["sink_c"])
    merged = (jax.nn.sigmoid(ga) * (o_a @ lp["w_oa"])
              + jax.nn.sigmoid(gb) * (o_b @ lp["w_ob"])
              + jax.nn.sigmoid(gc) * (o_c @ lp["w_oc"]))
    out = merged @ lp["w_out"]
    if is_ctx:
        return out, (kb, vb, kc, vc, jnp.stack([last_f, last_b], axis=1))
    return out, None


def conv_ffn(h, lp):
    gate, val = jnp.split(h @ lp["w_up"], 2, axis=-1)
    gate = dwconv(gate, lp["ffn_conv_w"], lp["ffn_conv_b"], FFN_CONV_PAD_L)
    return (jax.nn.gelu(gate) * val) @ lp["w_down"]


def layer(x, cond, lp, ctx):
    sh1, sc1, g1, sh2, sc2, g2 = adaln_params(cond, lp["w_mod"], lp["b_mod"])
    mix, ctx_out = token_mixers(modulate(x, lp["norm1_g"], sh1, sc1), lp, ctx)
    x = x + g1 * mix
    x = x + g2 * conv_ffn(modulate(x, lp["norm2_g"], sh2, sc2), lp)
    return x, ctx_out


def setup_inputs(seed: int = 0) -> dict:
    key = jax.random.key(seed)
    ks = jax.random.split(key, 40)

    def nrm(k, shape, scale):
        return jax.random.normal(k, shape, jnp.float32) * scale

    a0 = jax.random.uniform(ks[20], (DEPTH, 2, D_LRU), jnp.float32, 0.9, 0.999)
    return {
        "x_prompt": nrm(ks[0], (BATCH, SEQ, D_MODEL), 1.0),
        "x_sample": nrm(ks[1], (DEC_BATCH, DEC_SEQ, D_MODEL), 1.0),
        "c": nrm(ks[2], (DEC_BATCH, D_MODEL), 1.0),
        "cache_kb": nrm(ks[3], (DEC_BATCH, DEPTH, PAST_LEN, KV_HEADS_B, HEAD_DIM), 1.0),
        "cache_vb": nrm(ks[4], (DEC_BATCH, DEPTH, PAST_LEN, KV_HEADS_B, HEAD_DIM), 1.0),
        "cache_kc": nrm(ks[5], (DEC_BATCH, DEPTH, PAST_LEN, KV_HEADS_C, HEAD_DIM), 1.0),
        "cache_vc": nrm(ks[6], (DEC_BATCH, DEPTH, PAST_LEN, KV_HEADS_C, HEAD_DIM), 1.0),
        "state_lru": nrm(ks[7], (DEC_BATCH, DEPTH, 2, D_LRU), 0.5),
        "c_ctx": nrm(ks[8], (D_MODEL,), 1.0),
        "norm1_g": 1.0 + nrm(ks[9], (DEPTH, D_MODEL), 0.02),
        "norm2_g": 1.0 + nrm(ks[10], (DEPTH, D_MODEL), 0.02),
        "w_mod": nrm(ks[11], (DEPTH, D_MODEL, 6 * D_MODEL), 0.5 * D_MODEL ** -0.5),
        "b_mod": nrm(ks[12], (DEPTH, 6 * D_MODEL), 0.02),
        "w_in": nrm(ks[13], (DEPTH, D_MODEL, D_IN), D_MODEL ** -0.5),
        "lru_conv_w": nrm(ks[14], (DEPTH, LRU_CONV_W, D_LRU), LRU_CONV_W ** -0.5),
        "lru_conv_b": nrm(ks[15], (DEPTH, D_LRU), 0.02),
        "lru_wa": nrm(ks[16], (DEPTH, 2, LRU_BLOCKS, LRU_BLOCK, LRU_BLOCK), LRU_BLOCK ** -0.5),
        "lru_ba": nrm(ks[17], (DEPTH, 2, D_LRU), 0.02),
        "lru_wx": nrm(ks[18], (DEPTH, 2, LRU_BLOCKS, LRU_BLOCK, LRU_BLOCK), LRU_BLOCK ** -0.5),
        "lru_bx": nrm(ks[19], (DEPTH, 2, D_LRU), 0.02),
        "lru_lam": jnp.log(a0) - jnp.log1p(-a0),
        "qnorm_g": 1.0 + nrm(ks[21], (DEPTH, HEAD_DIM), 0.02),
        "knorm_g": 1.0 + nrm(ks[22], (DEPTH, HEAD_DIM), 0.02),
        "sink_c": nrm(ks[23], (DEPTH, HEADS_C), 0.5),
        "w_oa": nrm(ks[24], (DEPTH, D_LRU, D_MODEL), D_LRU ** -0.5),
        "w_ob": nrm(ks[25], (DEPTH, D_QB, D_MODEL), D_QB ** -0.5),
        "w_oc": nrm(ks[26], (DEPTH, D_QC, D_MODEL), D_QC ** -0.5),
        "w_out": nrm(ks[27], (DEPTH, D_MODEL, D_MODEL), D_MODEL ** -0.5),
        "w_up": nrm(ks[28], (DEPTH, D_MODEL, 2 * D_FF), D_MODEL ** -0.5),
        "ffn_conv_w": nrm(ks[29], (DEPTH, FFN_CONV_W, D_FF), FFN_CONV_W ** -0.5),
        "ffn_conv_b": nrm(ks[30], (DEPTH, D_FF), 0.02),
        "w_down": nrm(ks[31], (DEPTH, D_FF, D_MODEL), D_FF ** -0.5),
        "final_g": 1.0 + nrm(ks[32], (D_MODEL,), 0.02),
    }


def reference(x_prompt, x_sample, c, cache_kb, cache_vb, cache_kc, cache_vc, state_lru,
              c_ctx, norm1_g, norm2_g, w_mod, b_mod, w_in, lru_conv_w, lru_conv_b,
              lru_wa, lru_ba, lru_wx, lru_bx, lru_lam, qnorm_g, knorm_g, sink_c,
              w_oa, w_ob, w_oc, w_out, w_up, ffn_conv_w, ffn_conv_b, w_down, final_g):
    stacked = {
        "norm1_g": norm1_g, "norm2_g": norm2_g, "w_mod": w_mod, "b_mod": b_mod,
        "w_in": w_in, "lru_conv_w": lru_conv_w, "lru_conv_b": lru_conv_b,
        "lru_wa": lru_wa, "lru_ba": lru_ba, "lru_wx": lru_wx, "lru_bx": lru_bx,
        "lru_lam": lru_lam, "qnorm_g": qnorm_g, "knorm_g": knorm_g, "sink_c": sink_c,
        "w_oa": w_oa, "w_ob": w_ob, "w_oc": w_oc, "w_out": w_out, "w_up": w_up,
        "ffn_conv_w": ffn_conv_w, "ffn_conv_b": ffn_conv_b, "w_down": w_down,
    }
    cond_ctx = c_ctx[None, :]
    xp, xs = x_prompt, x_sample
    kbs, vbs, kcs, vcs, lrus = [], [], [], [], []
    for l in range(DEPTH):
        lp = {name: arr[l] for name, arr in stacked.items()}
        xp, (kb_l, vb_l, kc_l, vc_l, lru_l) = layer(xp, cond_ctx, lp, None)
        kbs.append(kb_l)
        vbs.append(vb_l)
        kcs.append(kc_l)
        vcs.append(vc_l)
        lrus.append(lru_l)
        cached = {"kb": cache_kb[:, l], "vb": cache_vb[:, l], "kc": cache_kc[:, l],
                  "vc": cache_vc[:, l], "state": state_lru[:, l]}
        xs, _ = layer(xs, c, lp, cached)
    y_prompt = rmsnorm(xp, final_g)
    y_sample = rmsnorm(xs, final_g)
    new_kb = jnp.stack(kbs, axis=1)
    new_vb = jnp.stack(vbs, axis=1)
    new_kc = jnp.stack(kcs, axis=1)
    new_vc = jnp.stack(vcs, axis=1)
    new_lru = jnp.stack(lrus, axis=1)
    return (y_prompt, y_sample, new_kb, new_vb, new_kc, new_vc, new_lru)
```

```python
import contextlib
import numpy as np
import ml_dtypes
import concourse.bass as bass
import concourse.mybir as mybir
from concourse.bass_utils import run_bass_kernel_spmd

F32 = mybir.dt.float32
BF16 = mybir.dt.bfloat16
ALU = mybir.AluOpType
AF = mybir.ActivationFunctionType

ENGS = ("pe", "act", "dve", "pool", "sp")
NDMASEM = 24
DEPTH = 2
T = 1024
EPS = 1e-6
GELU_C = 0.7978845608028654

C_XA, C_YA, C_QB, C_KB, C_VB, C_QC, C_KC, C_VC, C_GA, C_GB, C_GC = (
    0, 1024, 2048, 2560, 2688, 2816, 3328, 3456, 3584, 4608, 5632)

PRM = {}
_o = 0
for _n, _r in (("cond", 16), ("n1", 16), ("n2", 16), ("bmod", 96), ("lcw", 64), ("lcb", 16),
               ("ba", 32), ("bx", 32), ("lam", 32), ("fcw", 132), ("fcb", 44), ("fg", 8), ("st", 32)):
    PRM[_n] = _o
    _o += _r
PRM_ROWS = 640


class Res:
    __slots__ = ("buf", "w", "r")

    def __init__(self, buf):
        self.buf = buf
        self.w = None
        self.r = []


class Buf:
    def __init__(self, name, lo=None, hi=None, excl=False):
        self.name = name
        self.excl = excl
        self.lo = lo
        self.hi = hi
        self._res = {}
        self.ov = []
        self.acc = {}
        self.wr = {}
        self.gen = []

    def newgen(self):
        self.gen = [p for (_, p) in self.acc.values()]
        self._res = {}

    def r(self, *key):
        if key not in self._res:
            self._res[key] = Res(self)
        return self._res[key]

    def rs(self, *lists):
        out = []

        def rec(i, cur):
            if i == len(lists):
                out.append(self.r(*cur))
                return
            for v in lists[i]:
                rec(i + 1, cur + (v,))

        rec(0, ())
        return out


class Op:
    __slots__ = ("eng", "fn", "waits", "idx", "signal", "count", "dsem", "dval", "is_dma", "inc")


class Sched:
    def __init__(self, nc):
        self.nc = nc
        self.ops = {e: [] for e in ENGS}
        self.seen = {e: {} for e in ENGS}
        self.ndma = 0
        self.gcnt = {"sw": 0, "hw": 0, "cc": 0}
        self.dma_last = [None] * NDMASEM
        self.dma_cnt = [0] * NDMASEM
        self.bufs = []
        self.stopped = False

    def buf(self, name, lo=None, hi=None, excl=False):
        b = Buf(name, lo, hi, excl)
        if lo is not None:
            for o in self.bufs:
                if o.lo is not None and o.lo < hi and lo < o.hi:
                    o.ov.append(b)
                    b.ov.append(o)
        self.bufs.append(b)
        return b

    @staticmethod
    def _key(p):
        if p.is_dma:
            return ("d", p.dsem), p.dval
        return ("e", p.eng), p.idx

    def add(self, eng, fn, reads=(), writes=(), dma=False, inc=16):
        if self.stopped:
            return None
        op = Op()
        op.eng = eng
        op.fn = fn
        op.is_dma = dma
        op.signal = False
        op.count = None
        op.inc = inc
        op.idx = len(self.ops[eng])
        deps = []
        rb = set()
        wb = set()
        for r in reads:
            if r.w is not None:
                deps.append((r.w, True))
            if r.buf.excl:
                for rr in r.r:
                    deps.append((rr, False))
            rb.add(r.buf)
        for w in writes:
            if w.w is not None:
                deps.append((w.w, False))
            for rr in w.r:
                deps.append((rr, False))
            wb.add(w.buf)
        for b in rb | wb:
            for p in b.gen:
                deps.append((p, False))
        for b in rb:
            for o in b.ov:
                for (_, p) in o.wr.values():
                    deps.append((p, False))
        for b in wb:
            for o in b.ov:
                for (_, p) in o.acc.values():
                    deps.append((p, False))
        if dma:
            grp = "cc" if inc == 1 else ("sw" if eng == "pool" else "hw")
            lo_, n_ = {"sw": (0, 10), "hw": (10, 10), "cc": (20, 4)}[grp]
            s = lo_ + self.gcnt[grp] % n_
            self.gcnt[grp] += 1
            self.ndma += 1
            prev = self.dma_last[s]
            if prev is not None:
                deps.append((prev, True))
            self.dma_cnt[s] += inc
            op.dsem = s
            op.dval = self.dma_cnt[s]
            self.dma_last[s] = op
        waits = {}
        seen = self.seen[eng]
        for p, raw in deps:
            key, val = self._key(p)
            if not p.is_dma and p.eng == eng and eng == "pe":
                continue
            if seen.get(key, -1) >= val:
                continue
            if key not in waits or waits[key][0] < val:
                waits[key] = (val, p)
        for key, (val, p) in waits.items():
            seen[key] = val
            p.signal = True
        op.waits = [p for (_, p) in waits.values()]
        k, v = self._key(op)
        for r in reads:
            r.r.append(op)
        for w in writes:
            w.w = op
            w.r = []
        for b in rb | wb:
            if b.acc.get(k, (-1, None))[0] < v:
                b.acc[k] = (v, op)
        for b in wb:
            if b.wr.get(k, (-1, None))[0] < v:
                b.wr[k] = (v, op)
        self.ops[eng].append(op)
        return op

    def emit(self):
        nc = self.nc
        with contextlib.ExitStack() as st:
            esem = {e: st.enter_context(nc.semaphore(f"s_{e}")) for e in ENGS}
            dsem = [st.enter_context(nc.semaphore(f"s_dma{i}")) for i in range(NDMASEM)]
            for e in ENGS:
                c = 0
                for op in self.ops[e]:
                    if not op.is_dma and op.signal:
                        c += 1
                        op.count = c
            block = st.enter_context(nc.Block())

            def run(e, engobj):
                for op in self.ops[e]:
                    for p in op.waits:
                        if p.is_dma:
                            engobj.wait_ge(dsem[p.dsem], p.dval)
                        else:
                            engobj.wait_ge(esem[p.eng], p.count)
                    ins = op.fn(engobj)
                    if op.is_dma:
                        ins.then_inc(dsem[op.dsem], op.inc)
                    elif op.signal:
                        ins.then_inc(esem[e], 1)
                if e == "sp":
                    for s in range(NDMASEM):
                        if self.dma_cnt[s]:
                            engobj.wait_ge(dsem[s], self.dma_cnt[s])

            @block.tensor
            def _(eng):
                run("pe", eng)

            @block.scalar
            def _(eng):
                run("act", eng)

            @block.vector
            def _(eng):
                run("dve", eng)

            @block.gpsimd
            def _(eng):
                run("pool", eng)

            @block.sync
            def _(eng):
                run("sp", eng)


STOP = None
STAGES = []


def build(debug=False):
    nc = bass.Bass("TRN2", target_bir_lowering=False)
    S = Sched(nc)
    _stg = [0]

    def stage(name):
        _stg[0] += 1
        STAGES.append((name, len(S.ops["pe"])))
        hit = (STOP is not None) and ((_stg[0] >= STOP) if isinstance(STOP, int) else (name == STOP))
        if hit and not S.stopped:
            print("STOP at stage", _stg[0], name, flush=True)
            S.stopped = True

    def din(name, shape, dt=F32):
        return nc.dram_tensor(name, list(shape), dt, kind="ExternalInput").ap()

    def dout(name, shape, dt=F32):
        return nc.dram_tensor(name, list(shape), dt, kind="ExternalOutput").ap()

    def dint(name, shape, dt=F32):
        return nc.dram_tensor(name, list(shape), dt).ap()

    xin = din("xin", [2048, 1024])
    prm = din("prm", [PRM_ROWS, 128])
    w_mod = din("w_mod", [2, 1024, 6144])
    w_in = din("w_in", [2, 1024, 6656])
    lru_wa = din("lru_wa", [2, 2, 8, 128, 128])
    lru_wx = din("lru_wx", [2, 2, 8, 128, 128])
    w_oa = din("w_oa", [2, 1024, 1024])
    w_ob = din("w_ob", [2, 512, 1024])
    w_oc = din("w_oc", [2, 512, 1024])
    w_out = din("w_out", [2, 1024, 1024])
    w_up = din("w_up", [2, 1024, 5632])
    w_down = din("w_down", [2, 2816, 1024])
    qkg = din("qkg", [2, 2, 64])
    sinkc = din("sinkc", [2, 8])
    ckb = din("ckb", [2, 256, 128])
    cvb = din("cvb", [2, 256, 128])
    ckc = din("ckc", [2, 256, 128])
    cvc = din("cvc", [2, 256, 128])
    identd = din("identd", [128, 128])
    rotd = din("rotd", [128, 128])
    cosd = din("cosd", [128, 1024])
    sind = din("sind", [128, 1024])
    maskd = din("maskd", [4, 128, 128])
    seld = din("seld", [128, 16])

    y_out = dout("y_out", [2048, 1024])
    nkv = dout("nkv", [2, 1024, 512])
    nlru = dout("nlru", [2, 64, 128])

    kvd = [[[dint(f"kvd{l}_{s}_{m}", [128, 2560], BF16) for m in range(2)] for s in range(2)] for l in range(2)]
    ccA_out = [[dint(f"ccAo{l}_{m}", [512, 2560], BF16) for m in range(2)] for l in range(2)]
    ccB_in = [dint(f"ccBi{l}", [128, 24]) for l in range(2)]
    ccB_out = [dint(f"ccBo{l}", [512, 24]) for l in range(2)]
    ccC_in = [dint(f"ccCi{l}", [128, 32]) for l in range(2)]
    ccC_out = [dint(f"ccCo{l}", [512, 32]) for l in range(2)]
    ccD_in = [dint(f"ccDi{l}", [128, 44]) for l in range(2)]
    ccD_out = [dint(f"ccDo{l}", [512, 44]) for l in range(2)]
    RG = [[0, 1, 2, 3], [4, 5, 6, 7]]

    ARENA = 212480
    arena = nc.alloc_sbuf_tensor("arena", [128, ARENA // 4], F32)
    base = nc.lookup_mloc(arena).addr
    _cnt = [0]

    def at(shape, dt, off):
        _cnt[0] += 1
        return nc.alloc_sbuf_tensor_at(f"t{_cnt[0]}", list(shape), dt, offset=base + off)

    def nbytes(shape, dt):
        n = 1
        for s_ in shape[1:]:
            n *= s_
        return n * (2 if dt == BF16 else 4)

    _bump = [0]

    def fixed(name, shape, dt):
        off = _bump[0]
        sz = (nbytes(shape, dt) + 31) // 32 * 32
        _bump[0] += sz
        return at(shape, dt, off), S.buf(name, off, off + sz)

    def region(name, off, shape, dt):
        return at(shape, dt, off), S.buf(name, off, off + nbytes(shape, dt))

    xres, Bx = fixed("xres", [128, 8, 2048], F32)
    h, Bh = fixed("h", [128, 8, 1024], BF16)
    NSLOT = 3
    slot_off = []
    Bslot = []
    for i in range(NSLOT):
        slot_off.append(_bump[0])
        _, b = fixed(f"slot{i}", [128, 4096], BF16)
        Bslot.append(b)
    ident, Bid = fixed("ident", [128, 128], F32)
    rotb, Brot = fixed("rotb", [128, 128], BF16)
    onesb, Bones = fixed("onesb", [128, 128], BF16)
    onesblk, Boblk = fixed("onesblk", [128, 128], BF16)
    maskb, Bmask = fixed("maskb", [128, 4, 128], BF16)
    cosb, Bcos = fixed("cosb", [128, 1024], F32)
    sinb, Bsin = fixed("sinb", [128, 1024], F32)
    PT, BPT = fixed("PT", [128, PRM_ROWS], F32)
    mod, Bmod = fixed("mod", [128, 2, 48, 2], F32)
    AA, BAA = fixed("AA", [128, 2, 2, 3, 8], F32)
    cn, Bcn = fixed("cn", [128, 2, 32], F32)
    hbias, Bhb = fixed("hbias", [128, 2, 32], F32)
    sel, Bsel = fixed("sel", [128, 16], F32)
    es, Bes = fixed("es", [128, 16], F32)
    qkgc, Bqkg = fixed("qkgc", [128, 4], F32)
    kngbc, Bkng = fixed("kngbc", [128, 2, 64], F32)
    sc, Bsc = fixed("sc", [128, 8, 2], BF16)
    scf, Bscf = fixed("scf", [128, 8, 2], F32)
    lst, Blst = fixed("lst", [128, 4, 2, 8], F32)
    lstT, BlstT = fixed("lstT", [64, 128], F32)
    hst, Bhst = fixed("hst", [128, 2, 8], F32)
    smr, Bsmr = fixed("smr", [128, 2, 8, 2], F32)
    ccc, Bccc = fixed("ccc", [128, 2, 8, 2], F32)
    ccg, Bccg = fixed("ccg", [128, 4, 2, 8, 2], F32)
    carry, Bcar = fixed("carry", [128, 2, 8], F32)
    carryP, BcarP = fixed("carryP", [128, 2, 8], F32)
    HH, BHH = fixed("HH", [128, 2, 5, 8], F32)
    xe, Bxe = fixed("xe", [128, 8, 3], F32)
    xg, Bxg = fixed("xg", [128, 4, 8, 3], F32)
    eg, Beg = fixed("eg", [128, 22, 4], F32)
    ev, Bev = fixed("ev", [128, 22, 2], F32)
    ge, Bge = fixed("ge", [128, 22, 2], F32)
    gg, Bgg = fixed("gg", [128, 4, 22, 2], F32)
    ehal, Behal = fixed("ehal", [128, 2, 22], F32)
    etmp, Betmp = fixed("etmp", [128, 4, 22], F32)
    epsc, Beps = fixed("epsc", [128, 1], F32)
    junk, Bjunk = fixed("junk", [128, 64], F32)
    ssk, Bssk = fixed("ssk", [128, 4], F32)
    assert _bump[0] <= 126976, _bump[0]
    XO, YO, ZO = 126976, 160256, 176640
    ZEND = ARENA

    def slot_view(i, shape):
        return at(shape, BF16, slot_off[i])

    slotv = {}

    def sv(i, shape):
        k = (i, tuple(shape))
        if k not in slotv:
            slotv[k] = slot_view(i, shape)
        return slotv[k]

    _slot_rr = [0]

    def next_slot():
        i = _slot_rr[0] % NSLOT
        _slot_rr[0] += 1
        Bslot[i].newgen()
        return i

    pb2 = [nc.alloc_psum_tensor(f"pb{i}", [128, 1024], F32) for i in range(4)]
    bank = [pb2[b // 2][:, (b % 2) * 512:(b % 2 + 1) * 512] for b in range(8)]
    Bbank = [S.buf(f"bank{b}", excl=True) for b in range(8)]
    pmm = bank[0:4]
    Bpmm = Bbank[0:4]
    pacc = bank[4:6]
    Bpacc = Bbank[4:6]
    pbig = pb2[3]
    _rr = {"mm": 0, "acc": 0}

    def ps_mm():
        i = _rr["mm"] % 8
        _rr["mm"] += 1
        return bank[i], Bbank[i].r()

    def ps_acc():
        i = _rr["acc"] % 2
        _rr["acc"] += 1
        return pacc[i], Bpacc[i].r()

    def ACT(out, in_, func, R, W, bias=0.0, scale=1.0, accum=None):
        kw = {}
        if accum is not None:
            kw["accum_out"] = accum
        S.add("act", lambda e: e.activation(out=out, in_=in_, func=func, bias=bias, scale=scale, **kw), reads=R, writes=W)

    def TT(eng, out, a, b, op, R, W):
        S.add(eng, lambda e: e.tensor_tensor(out=out, in0=a, in1=b, op=op), reads=R, writes=W)

    def TS(eng, out, a, s1, op0, R, W, s2=None, op1=None):
        if op1 is None:
            S.add(eng, lambda e: e.tensor_scalar(out=out, in0=a, scalar1=s1, scalar2=None, op0=op0), reads=R, writes=W)
        else:
            S.add(eng, lambda e: e.tensor_scalar(out=out, in0=a, scalar1=s1, scalar2=s2, op0=op0, op1=op1), reads=R, writes=W)

    def STT(eng, out, a, scal, b, op0, op1, R, W):
        S.add(eng, lambda e: e.scalar_tensor_tensor(out=out, in0=a, scalar=scal, in1=b, op0=op0, op1=op1), reads=R, writes=W)

    def CP(eng, out, in_, R, W):
        if eng == "act":
            S.add("act", lambda e: e.copy(out=out, in_=in_), reads=R, writes=W)
        else:
            S.add(eng, lambda e: e.tensor_copy(out=out, in_=in_), reads=R, writes=W)

    def MSET(eng, ap, val, W):
        S.add(eng, lambda e: e.memset(ap, val), writes=W)

    def MM(out, lhsT, rhs, start, stop, R, W):
        S.add("pe", lambda e: e.matmul(out, lhsT, rhs, start=start, stop=stop), reads=R, writes=W)

    def TR(out, in_, R, W):
        S.add("pe", lambda e: e.transpose(out, in_, ident[:]), reads=R + [Bid.r()], writes=W)

    def DMA(q, out, in_, R, W):
        S.add(q, lambda e: e.dma_start(out=out, in_=in_), reads=R, writes=W, dma=True)

    def AG(in_ap, out_ap, R, W):
        S.add("pool", lambda e: e.collective_compute("AllGather", ALU.bypass, replica_groups=RG, ins=[in_ap], outs=[out_ap]),
              reads=R, writes=W, dma=True, inc=1)

    def load_w(src, kch, ncols, parts=1):
        i = next_slot()
        v = sv(i, [128, kch, ncols])
        DMA("pool", v[:], src.rearrange("(k p) n -> p k n", p=128), [], [Bslot[i].r(0)])
        return i, v, [Bslot[i].r(0)]

    DMA("sp", ident[:], identd, [], [Bid.r()])
    DMA("pool", rotb[:], rotd, [], [Brot.r()])
    DMA("pool", maskb[:], maskd.rearrange("m p n -> p m n"), [], [Bmask.r()])
    DMA("sp", cosb[:], cosd, [], [Bcos.r()])
    DMA("sp", sinb[:], sind, [], [Bsin.r()])
    DMA("sp", sel[:], seld, [], [Bsel.r()])
    MSET("dve", onesb[:], 1.0, [Bones.r()])
    MSET("dve", onesblk[:], 0.0, [Boblk.r()])
    MSET("dve", onesblk[0:64, 0:64], 1.0, [Boblk.r()])
    MSET("dve", onesblk[64:128, 64:128], 1.0, [Boblk.r()])
    MSET("dve", epsc[:], EPS, [Beps.r()])
    with nc.allow_non_contiguous_dma(reason="tiny param loads"):
        for l in range(2):
            for qk in range(2):
                for half in range(2):
                    src = bass.AP(qkg.tensor, (l * 2 + qk) * 64, [[1, 64], [1, 1]])
                    DMA("sp", qkgc[half * 64:(half + 1) * 64, l * 2 + qk:l * 2 + qk + 1], src, [], [Bqkg.r(l, qk, half)])
            src = bass.AP(qkg.tensor, (l * 2 + 1) * 64, [[0, 128], [1, 64]])
            DMA("sp", kngbc[:, l, :], src, [], [Bkng.r(l)])
        src = bass.AP(sinkc.tensor, 0, [[0, 128], [1, 16]])
        DMA("sp", es[:], src, [], [Bes.r()])
    ACT(es[:], es[:], AF.Exp, [Bes.r()], [Bes.r()])

    pstage, Bpst = region("pstage", ZO, [128, 5, 128], F32)
    DMA("sp", pstage[:], prm.rearrange("(g r) c -> r g c", r=128), [], [Bpst.r()])
    pa, ra = ps_mm()
    for g in range(4):
        TR(pa[:, g * 128:(g + 1) * 128], pstage[:, g, :], [Bpst.r()], [ra])
    CP("dve", PT[:, 0:512], pa[:, 0:512], [ra], [BPT.r()])
    pa, ra = ps_mm()
    TR(pa[:, 0:128], pstage[:, 4, :], [Bpst.r()], [ra])
    CP("dve", PT[:, 512:640], pa[:, 0:128], [ra], [BPT.r()])
    RPT = [BPT.r()]

    for c in range(2):
        ACT(sc[:, :, c], PT[:, PRM["cond"] + c * 8:PRM["cond"] + c * 8 + 8], AF.Silu, RPT, [Bsc.r()])
        ACT(scf[:, :, c], PT[:, PRM["cond"] + c * 8:PRM["cond"] + c * 8 + 8], AF.Silu, RPT, [Bscf.r()])
    lamv = PT[:, PRM["lam"]:PRM["lam"] + 32]
    ACT(cn[:, 0, :], lamv, AF.Exp, RPT, [Bcn.r()], scale=-1.0)
    ACT(cn[:, 0, :], cn[:, 0, :], AF.Ln, [Bcn.r()], [Bcn.r()], bias=1.0)
    TS("dve", cn[:, 1, :], cn[:, 0, :], -8.0, ALU.mult, [Bcn.r()], [Bcn.r()])
    TS("dve", cn[:, 0, :], cn[:, 0, :], -4.0, ALU.mult, [Bcn.r()], [Bcn.r()])
    TS("dve", hbias[:, 0, :], PT[:, PRM["ba"]:PRM["ba"] + 32], 0.5, ALU.mult, RPT, [Bhb.r()])
    TS("dve", hbias[:, 1, :], PT[:, PRM["bx"]:PRM["bx"] + 32], 0.5, ALU.mult, RPT, [Bhb.r()])

    stage('adaln')
    wf32 = [region(f"wf32_{i}", XO + i * 16384, [128, 8, 512], F32) for i in range(2)]
    for l in range(2):
        pm, rm = ps_mm()
        for blk in range(12):
            if blk % 2 == 0:
                i, v, rw = load_w(w_mod[l, :, blk * 512:(blk + 1) * 512], 8, 512)
                rhs_, rhsR = sc, Bsc.r()
            else:
                v, Bwf = wf32[(blk // 2) % 2]
                DMA("sp", v[:], w_mod[l, :, blk * 512:(blk + 1) * 512].rearrange("(k p) n -> p k n", p=128), [], [Bwf.r()])
                rw = [Bwf.r()]
                rhs_, rhsR = scf, Bscf.r()
            for fc in range(4):
                f = blk * 4 + fc
                for kc in range(8):
                    MM(pm[:, f * 2:f * 2 + 2], v[:, kc, fc * 128:(fc + 1) * 128], rhs_[:, kc, :], kc == 0, kc == 7,
                       rw + [rhsR], [rm])
        for c in range(2):
            TT("dve", mod[:, l, :, c], pm[:, c:96:2], PT[:, PRM["bmod"] + l * 48:PRM["bmod"] + l * 48 + 48], ALU.add,
               [rm] + RPT, [Bmod.r()])
        for s in range(2):
            c = 0 if s == 0 else 1
            STT("dve", AA[:, l, s, 0, :], mod[:, l, 8:16, c], 1.0, PT[:, PRM["n1"] + l * 8:PRM["n1"] + l * 8 + 8],
                ALU.add, ALU.mult, [Bmod.r()] + RPT, [BAA.r()])
            STT("dve", AA[:, l, s, 1, :], mod[:, l, 32:40, c], 1.0, PT[:, PRM["n2"] + l * 8:PRM["n2"] + l * 8 + 8],
                ALU.add, ALU.mult, [Bmod.r()] + RPT, [BAA.r()])
            TS("dve", AA[:, l, s, 2, :], mod[:, l, 16:24, c], 0.5, ALU.mult, [Bmod.r()], [BAA.r()])
    RC = [BAA.r(), Bmod.r()] + RPT

    stage('xload')
    xst = [region(f"xst{i}", ZO + 4096 + i * 4096, [128, 1024], F32) for i in range(2)]
    for tt in range(16):
        xs_, Bxs = xst[tt % 2]
        DMA("sp", xs_[:], xin[tt * 128:(tt + 1) * 128, :], [], [Bxs.r()])
        s, tile = tt // 8, (tt % 8) // 4
        for g in range(2):
            pa, ra = ps_mm()
            for j in range(4):
                kc = g * 4 + j
                TR(pa[:, j * 128:(j + 1) * 128], xs_[:, kc * 128:(kc + 1) * 128], [Bxs.r()], [ra])
            eng = "dve" if g == 0 else "act"
            CP(eng, xres[:, g * 4:g * 4 + 4, tt * 128:(tt + 1) * 128], pa[:, 0:512].rearrange("p (k n) -> p k n", k=4),
               [ra], Bx.rs([s], range(g * 4, g * 4 + 4), [tile]))

    nsq, Bnsq = region("nsq", ZO, [128, 8, 512], BF16)
    nrs, Bnrs = region("nrs", ZO + 8192, [128, 512], F32)
    ntm = [region(f"ntm{i}", ZO + 10240 + i * 2048, [128, 512], F32) for i in range(2)]

    def rmsnorm_h(s, acol, bcol):
        for tile in range(2):
            tok = slice(s * T + tile * 512, s * T + (tile + 1) * 512)
            xr = Bx.rs([s], range(8), [tile])
            ACT(nsq[:], xres[:, :, tok], AF.Square, xr, [Bnsq.r()])
            pa, ra = ps_mm()
            for kc in range(8):
                MM(pa[:], onesb[:], nsq[:, kc, :], kc == 0, kc == 7, [Bones.r(), Bnsq.r()], [ra])
            ACT(nrs[:], pa[:], AF.Sqrt, [ra, Beps.r()], [Bnrs.r()], bias=epsc[:], scale=1.0 / 1024)
            S.add("dve", lambda e: e.reciprocal(out=nrs[:], in_=nrs[:]), reads=[Bnrs.r()], writes=[Bnrs.r()])
            for kc in range(8):
                tm, Btm = ntm[kc % 2]
                TT("dve", tm[:], xres[:, kc, tok], nrs[:], ALU.mult, [Bx.r(s, kc, tile), Bnrs.r()], [Btm.r()])
                ACT(h[:, kc, tile * 512:(tile + 1) * 512], tm[:], AF.Identity, [Btm.r()] + RC, [Bh.r(kc, tile)],
                    bias=bcol(kc), scale=acol(kc))

    def hR(tile):
        return Bh.rs(range(8), [tile])

    qsq, Bqsq = region("qsq", YO + 8192, [128, 512], BF16)
    qrs, Bqrs = region("qrs", YO + 9216, [128, 512], F32)
    qn, Bqn = region("qn", YO + 11264, [128, 512], F32)
    qnb, Bqnb = region("qnb", YO + 13312, [128, 512], BF16)
    qt1, Bqt1 = region("qt1", YO + 14336, [128, 512], F32)

    scr_sets = [((qsq, Bqsq), (qrs, Bqrs), (qn, Bqn), (qnb, Bqnb), (qt1, Bqt1)),
                (region("qsqZ", ZO, [128, 512], BF16), region("qrsZ", ZO + 1024, [128, 512], F32), region("qnZ", ZO + 3072, [128, 512], F32),
                 region("qnbZ", ZO + 5120, [128, 512], BF16), region("qt1Z", ZO + 6144, [128, 512], F32))]
    QS = [0]

    def qk_epilogue(pa, ra, dst, dstR, l, normcol, rope, tile):
        (qsq, Bqsq), (qrs, Bqrs), (qn, Bqn), (qnb, Bqnb), (qt1, Bqt1) = scr_sets[QS[0]]
        tk = slice(tile * 512, (tile + 1) * 512)
        if normcol is not None:
            ACT(qsq[:], pa[:], AF.Square, [ra], [Bqsq.r()])
            pb, rb = ps_mm()
            MM(pb[:], onesblk[:], qsq[:], True, True, [Boblk.r(), Bqsq.r()], [rb])
            ACT(qrs[:], pb[:], AF.Sqrt, [rb, Beps.r()], [Bqrs.r()], bias=epsc[:], scale=1.0 / 64)
            S.add("dve", lambda e: e.reciprocal(out=qrs[:], in_=qrs[:]), reads=[Bqrs.r()], writes=[Bqrs.r()])
            if not rope:
                STT("dve", dst, pa[:], normcol, qrs[:], ALU.mult, ALU.mult, [ra, Bqrs.r(), Bqkg.r(l, 0, 0), Bqkg.r(l, 0, 1), Bqkg.r(l, 1, 0), Bqkg.r(l, 1, 1)], dstR)
                return
            STT("dve", qn[:], pa[:], normcol, qrs[:], ALU.mult, ALU.mult, [ra, Bqrs.r(), Bqkg.r(l, 0, 0), Bqkg.r(l, 0, 1), Bqkg.r(l, 1, 0), Bqkg.r(l, 1, 1)], [Bqn.r()])
            src, srcR = qn[:], Bqn.r()
        else:
            if not rope:
                CP("act", dst, pa[:], [ra], dstR)
                return
            src, srcR = pa[:], ra
        stage('rope0')
        CP("act", qnb[:], src, [srcR], [Bqnb.r()])
        stage('rope1')
        pb, rb = ps_mm()
        MM(pb[:], rotb[:], qnb[:], True, True, [Brot.r(), Bqnb.r()], [rb])
        stage('rope2')
        TT("dve", qt1[:], src, cosb[:, tk], ALU.mult, [srcR, Bcos.r()], [Bqt1.r()])
        stage('rope3')
        TT("dve", qn[:], pb[:], sinb[:, tk], ALU.mult, [rb, Bsin.r()], [Bqn.r()])
        stage('rope4')
        TT("dve", dst, qt1[:], qn[:], ALU.add, [Bqt1.r(), Bqn.r()], dstR)
        stage('rope5')

    kvst, Bkvst = region("kvst", ZO + 16384, [128, 5120], BF16)
    kvout = [region(f"kvout{i}", ZO + 26624 + i * 2048, [128, 512], F32) for i in range(2)]
    xaP = at([128, 8, 4, 259], F32, XO)
    xaS = at([128, 8, 1, 1027], F32, XO)
    xaG = at([128, 8, 2054], BF16, XO)
    BxaPl = [S.buf(f"xaP{n}", XO + n * 4144, XO + (n + 1) * 4144) for n in range(8)]
    BxaSl = [S.buf(f"xaS{n}", XO + n * 4108, XO + (n + 1) * 4108) for n in range(8)]
    BxaGl = [S.buf(f"xaG{n}", XO + n * 4108, XO + (n + 1) * 4108) for n in range(8)]

    class _XaR:
        def __init__(self):
            self.s = 0

        def r(self, n, k):
            return (BxaPl if self.s == 0 else BxaSl)[n].r(k)

        def rs(self, ns, ks):
            return [self.r(n, k) for n in ns for k in ks]
    Bxa = _XaR()

    def kv_pass(l, s):
        i = next_slot()
        v = sv(i, [128, 8, 512])
        DMA("pool", v[:, :, 0:256], w_in[l, :, C_KB:C_KB + 256].rearrange("(k p) n -> p k n", p=128), [], [Bslot[i].r(0)])
        DMA("pool", v[:, :, 256:512], w_in[l, :, C_KC:C_KC + 256].rearrange("(k p) n -> p k n", p=128), [], [Bslot[i].r(1)])
        rw = [Bslot[i].r(0), Bslot[i].r(1)]
        vvb = kvst[:, 1024:2560].rearrange("p (b c) -> p b c", c=192)
        vvc = kvst[:, 3584:5120].rearrange("p (b c) -> p b c", c=192)
        MSET("dve", vvb[:, :, 64:128], 1.0, [Bkvst.r("ob")])
        MSET("dve", vvc[:, :, 64:128], 1.0, [Bkvst.r("oc")])
        stage('kv_a')
        for tt in range(8):
            tile = tt // 4
            pa, ra = ps_mm()
            for kc in range(8):
                MM(pa[:], h[:, kc, tt * 128:(tt + 1) * 128], v[:, kc, :], kc == 0, kc == 7, rw + [Bh.r(kc, tile)], [ra])
            CP("dve", vvb[:, tt, :].rearrange("p (a c) -> p a c", c=64)[:, 0:3:2, :],
               pa[:, 128:256].rearrange("p (a c) -> p a c", c=64), [ra], [Bkvst.r("vb", tt)])
            CP("dve", vvc[:, tt, :].rearrange("p (a c) -> p a c", c=64)[:, 0:3:2, :],
               pa[:, 384:512].rearrange("p (a c) -> p a c", c=64), [ra], [Bkvst.r("vc", tt)])
            if s == 0:
                ko, Bko = kvout[tt % 2]
                MSET("dve", ssk[:, 0:2], 0.0, [Bssk.r(0), Bssk.r(1)])
                for hd in range(2):
                    ACT(junk[:], pa[:, hd * 64:(hd + 1) * 64], AF.Square, [ra, Bssk.r(hd)], [Bjunk.r(), Bssk.r(hd)], accum=ssk[:, hd:hd + 1])
                ACT(ssk[:, 2:4], ssk[:, 0:2], AF.Sqrt, [Bssk.r(0), Bssk.r(1), Beps.r()], [Bssk.r(2)], bias=epsc[:], scale=1.0 / 64)
                S.add("dve", lambda e: e.reciprocal(out=ssk[:, 2:4], in_=ssk[:, 2:4]), reads=[Bssk.r(2)], writes=[Bssk.r(2)])
                for hd in range(2):
                    STT("dve", ko[:, hd * 64:(hd + 1) * 64], pa[:, hd * 64:(hd + 1) * 64], ssk[:, 2 + hd:3 + hd], kngbc[:, l, :],
                        ALU.mult, ALU.mult, [ra, Bssk.r(2), Bkng.r(l)], [Bko.r()])
                CP("act", ko[:, 128:512], pa[:, 128:512], [ra], [Bko.r()])
                DMA("sp", nkv[l, tt * 128:(tt + 1) * 128, :], ko[:], [Bko.r()], [])
        stage('kv_b')
        for (c0, dcol, norm) in ((0, 0, True), (256, 2560, False)):
            for tile in range(2):
                pa, ra = ps_mm()
                for kc in range(8):
                    MM(pa[:], v[:, kc, c0:c0 + 128], h[:, kc, tile * 512:(tile + 1) * 512], kc == 0, kc == 7,
                       rw + [Bh.r(kc, tile)], [ra])
                qk_epilogue(pa, ra, kvst[:, dcol + tile * 512:dcol + (tile + 1) * 512], [Bkvst.r("k", dcol, tile)],
                            l, qkgc[:, l * 2 + 1:l * 2 + 2] if norm else None, s == 1, tile)
        stage('kv_c')
        allr = ([Bkvst.r("ob"), Bkvst.r("oc")] + Bkvst.rs(["vb", "vc"], range(8)) + Bkvst.rs(["k"], [0, 2560], [0, 1]))
        Bd = [S.buf(f"kvd{l}{s}{m}") for m in range(2)]
        for m in range(2):
            DMA("sp", kvd[l][s][m], kvst[:, m * 2560:(m + 1) * 2560], allr, [Bd[m].r()])
        stage('kv_d')
        return Bd

    lset = [[region(f"l{i}_{j}", ZO + (i * 4 + j) * 2048, [128, 512], F32) for j in range(4)] for i in range(2)]
    gyb3 = [(at([128, 1024], BF16, slot_off[i] + 4096), S.buf(f"gyb3_{i}", slot_off[i] + 4096, slot_off[i] + 6144)) for i in range(NSLOT)]
    lset3 = [[(at([128, 512], F32, slot_off[i] + j * 2048), S.buf(f"l3_{i}_{j}", slot_off[i] + j * 2048, slot_off[i] + (j + 1) * 2048))
              for j in range(4)] for i in range(NSLOT)]
    xc = [region(f"xc{i}", ZO + 16384 + i * 4096, [128, 1024], F32) for i in range(2)]
    xcb = [region(f"xcb{i}", ZO + 24576 + i * 2048, [128, 1024], BF16) for i in range(2)]
    gyb, Bgyb = region("gyb", ZO + 28672, [128, 1024], BF16)
    HF = [region(f"HF{i}", ZO + 30720, [128, 1024], F32) for i in range(1)]
    obuf, Bo = region("obuf", YO, [128, 8, 1024], BF16)

    def lru(l, s):
        nseq, L = (4, 256) if s == 0 else (1, 1024)
        xa = xaP if s == 0 else xaS
        wi = next_slot()
        wv = sv(wi, [128, 4, 8, 128])
        for d in range(2):
            DMA("pool", wv[:, 2 * d, :, :], lru_wa[l, d].rearrange("n c d -> c n d"), [], [Bslot[wi].r(2 * d)])
            DMA("pool", wv[:, 2 * d + 1, :, :], lru_wx[l, d].rearrange("n c d -> c n d"), [], [Bslot[wi].r(2 * d + 1)])
        yi = next_slot()
        yarea = [sv(yi, [128, 2, 8, 128])[:, k_] for k_ in range(2)]
        li = next_slot()
        sets3 = [lset[0], lset[1], lset3[li]]
        gy2 = [(gyb, Bgyb), gyb3[yi]]

        def prologue(n):
            xcn, Bxc = xc[n % 2]
            xbn, Bxb = xcb[n % 2]
            gy, Bgy = gy2[n % 2]
            xaR = Bxa.rs([n], [0, 1, "h"])
            cw = lambda k: PT[:, PRM["lcw"] + (l * 4 + k) * 8 + n:PRM["lcw"] + (l * 4 + k) * 8 + n + 1]
            cb = PT[:, PRM["lcb"] + l * 8 + n:PRM["lcb"] + l * 8 + n + 1]
            xcv = xcn[:].rearrange("p (a b) -> p a b", a=nseq)
            TS("dve", xcv, xa[:, n, :, 0:L], cw(0), ALU.mult, xaR + RPT, [Bxc.r()], s2=cb, op1=ALU.add)
            for k in range(1, 4):
                STT("dve", xcv, xa[:, n, :, k:k + L], cw(k), xcv, ALU.mult, ALU.add, xaR + RPT + [Bxc.r()], [Bxc.r()])
            CP("act", xbn[:], xcn[:], [Bxc.r()], [Bxb.r()])
            yv = yarea[n % 2]
            yr = [Bslot[yi].r(n % 2)]
            DMA("pool", yv, w_in[l, :, C_YA + n * 128:C_YA + (n + 1) * 128].rearrange("(k p) n -> p k n", p=128), [], yr)
            for tile in range(2):
                pa, ra = ps_mm()
                for kc in range(8):
                    MM(pa[:], yv[:, kc, :], h[:, kc, tile * 512:(tile + 1) * 512], kc == 0, kc == 7,
                       yr + [Bh.r(kc, tile)], [ra])
                ACT(gy[:, tile * 512:(tile + 1) * 512], pa[:], AF.Gelu_apprx_tanh, [ra], [Bgy.r(tile)])

        def front(n, d, hi, st_):
            xbn, Bxb = xcb[n % 2]
            ci = (l * 2 + d) * 8 + n
            half = hi if d == 0 else 1 - hi
            tk = slice(half * 512, (half + 1) * 512)
            (Rt, BR), (It, BI), (At, BA), (St, BS) = st_
            pr, rr = ps_mm()
            MM(pr[:], wv[:, 2 * d, n, :], xbn[:, tk], True, True, [Bslot[wi].r(2 * d), Bxb.r()], [rr])
            pi, ri = ps_mm()
            MM(pi[:], wv[:, 2 * d + 1, n, :], xbn[:, tk], True, True, [Bslot[wi].r(2 * d + 1), Bxb.r()], [ri])
            ACT(Rt[:], pr[:], AF.Tanh, [rr, Bhb.r()], [BR.r()], bias=hbias[:, 0, ci:ci + 1], scale=0.5)
            ACT(It[:], pi[:], AF.Tanh, [ri, Bhb.r()], [BI.r()], bias=hbias[:, 1, ci:ci + 1], scale=0.5)
            ACT(At[:], Rt[:], AF.Exp, [BR.r(), Bcn.r()], [BA.r()], bias=cn[:, 0, ci:ci + 1], scale=cn[:, 0, ci:ci + 1])
            ACT(St[:], Rt[:], AF.Exp, [BR.r(), Bcn.r()], [BS.r()], bias=cn[:, 1, ci:ci + 1], scale=cn[:, 1, ci:ci + 1])

        def mid(st_):
            (Rt, BR), (It, BI), (At, BA), (St, BS) = st_
            ACT(St[:], St[:], AF.Sqrt, [BS.r()], [BS.r()], bias=0.25, scale=-0.25)

        def back(n, d, hi, st_):
            xcn, Bxc = xc[n % 2]
            hf, Bhf = HF[0]
            gy, Bgy = gy2[n % 2]
            xaR = Bxa.rs([n], [0, 1, "h"])
            half = hi if d == 0 else 1 - hi
            tk = slice(half * 512, (half + 1) * 512)
            (Rt, BR), (It, BI), (At, BA), (St, BS) = st_
            if True:
                if True:
                    STT("dve", It[:], It[:], 1.0, xcn[:, tk], ALU.add, ALU.mult, [BI.r(), Bxc.r()], [BI.r()])
                    TT("dve", It[:], It[:], St[:], ALU.mult, [BI.r(), BS.r()], [BI.r()])
                    segs = [(q, (q % 2) * 256, 256) for q in range(4) if q // 2 == half] if s == 0 else [(0, 0, 512)]
                    for (q, lo, n_) in segs:
                        if s == 0 or hi == 0:
                            init, initR = 0.0, []
                            pinit, pinitR = 1.0, []
                        else:
                            init, initR = carry[:, d, n:n + 1], [Bcar.r(d, n)]
                            pinit, pinitR = carryP[:, d, n:n + 1], [BcarP.r(d, n)]
                        if d == 0:
                            outap = hf[:, half * 512 + lo:half * 512 + lo + n_]
                            a_ap, u_ap = At[:, lo:lo + n_], It[:, lo:lo + n_]
                            outR = Bhf.r(half)
                            p_out = St[:, lo:lo + n_]
                        else:
                            outap = Rt[:, lo:lo + n_][:, ::-1]
                            a_ap, u_ap = At[:, lo:lo + n_][:, ::-1], It[:, lo:lo + n_][:, ::-1]
                            outR = BR.r()
                            p_out = St[:, lo:lo + n_][:, ::-1]
                        S.add("dve", lambda e, o=outap, a=a_ap, u=u_ap, i0=init: e.tensor_tensor_scan(
                            out=o, data0=a, data1=u, initial=i0, op0=ALU.mult, op1=ALU.add),
                            reads=[BA.r(), BI.r()] + initR, writes=[outR])
                        if s == 0:
                            if d == 0:
                                CP("dve", lst[:, q, 0, n:n + 1], hf[:, q * 256 + 255:q * 256 + 256], [outR], [Blst.r(q, 0, n)])
                            else:
                                CP("dve", lst[:, q, 1, n:n + 1], Rt[:, lo:lo + 1], [outR], [Blst.r(q, 1, n)])
                        else:
                            S.add("dve", lambda e, o=p_out, a=a_ap, i0=pinit: e.tensor_tensor_scan(
                                out=o, data0=a, data1=a, initial=i0, op0=ALU.mult, op1=ALU.min),
                                reads=[BA.r()] + pinitR, writes=[BS.r()])
                            hsrc = hf[:, half * 512 + 511:half * 512 + 512] if d == 0 else Rt[:, 0:1]
                            psrc = St[:, 511:512] if d == 0 else St[:, 0:1]
                            if hi == 0:
                                CP("dve", carry[:, d, n:n + 1], hsrc, [outR], [Bcar.r(d, n)])
                                CP("dve", carryP[:, d, n:n + 1], psrc, [BS.r()], [BcarP.r(d, n)])
                            else:
                                CP("dve", ccc[:, d, n, 1:2], hsrc, [outR], [Bccc.r(d, n, 1)])
                                CP("dve", ccc[:, d, n, 0:1], psrc, [BS.r()], [Bccc.r(d, n, 0)])
                            TT("dve", xaG[:, n, d * 1024 + half * 512:d * 1024 + (half + 1) * 512], St[:], gy[:, tk], ALU.mult,
                               [BS.r(), Bgy.r(half)] + xaR, [BxaGl[n].r(d, half)])
                    if d == 1:
                        TT("dve", Rt[:], Rt[:], hf[:, tk], ALU.add, [BR.r(), Bhf.r(half)], [BR.r()])
                        TT("dve", obuf[:, n, tk], Rt[:], gy[:, tk], ALU.mult, [BR.r(), Bgy.r(half)], [Bo.r(n, half)])


        items = [(n, d, hi) for n in range(8) for d in range(2) for hi in range(2)]
        prologue(0)
        k = 0
        while k < len(items):
            pair = items[k:k + 2]
            sts = [sets3[(k + i_) % 3] for i_ in range(len(pair))]
            for (n_, d_, hi_), st_ in zip(pair, sts):
                front(n_, d_, hi_, st_)
            for st_ in sts:
                mid(st_)
            if pair[0][1] == 0 and pair[0][0] + 1 < 8:
                prologue(pair[0][0] + 1)
            for (n_, d_, hi_), st_ in zip(pair, sts):
                back(n_, d_, hi_, st_)
            k += 2

    def lru_fix(l):
        for n in range(8):
            for d in range(2):
                for half in range(2):
                    tk = slice(half * 512, (half + 1) * 512)
                    STT("dve", obuf[:, n, tk], xaG[:, n, d * 1024 + half * 512:d * 1024 + (half + 1) * 512], hst[:, d, n:n + 1], obuf[:, n, tk],
                        ALU.mult, ALU.add, [BxaGl[n].r(d, half), Bhst.r(), Bo.r(n, half)], [Bo.r(n, half)])

    merged, Bmg = region("merged", XO, [128, 8, 1024], F32)
    mergedb, Bmgb = region("mergedb", ZO, [128, 8, 1024], BF16)
    pth = [region(f"pth{i}", ZO + 16384 + i * 2048, [128, 512], F32) for i in range(2)]
    ptm = [region(f"ptm{i}", ZO + 20480 + i * 2048, [128, 512], F32) for i in range(2)]

    pthA = [region(f"pthA{i}", ZO + 8192 + i * 2048, [128, 512], F32) for i in range(2)]

    def branch_proj(l, which, w_o, kch, gcol, ochunks):
        cnt = 0
        pth_ = pthA if which == 0 else pth
        for ob in range(2):
            if kch == 8:
                oi, ov, orr = load_w(w_o[l, :, ob * 512:(ob + 1) * 512], 8, 512)
            else:
                oi = next_slot()
                ov = sv(oi, [128, 4, 512])
                for half in range(2):
                    DMA("pool", ov[half * 64:(half + 1) * 64, :, :],
                        w_o[l, half * 256:(half + 1) * 256, ob * 512:(ob + 1) * 512].rearrange("(j p) n -> p j n", p=64),
                        [], [Bslot[oi].r(half)])
                orr = [Bslot[oi].r(0), Bslot[oi].r(1)]
            gi, gv, gr = load_w(w_in[l, :, gcol + ob * 512:gcol + (ob + 1) * 512], 8, 512)
            for o4 in range(4):
                oc = ob * 4 + o4
                for tile in range(2):
                    tk = slice(tile * 512, (tile + 1) * 512)
                    po, ro = ps_mm()
                    for kc in range(kch):
                        MM(po[:], ov[:, kc, o4 * 128:(o4 + 1) * 128], obuf[:, ochunks[kc], tk], kc == 0, kc == kch - 1,
                           orr + [Bo.r(ochunks[kc], tile)], [ro])
                    pg, rg = ps_mm()
                    for kc in range(8):
                        MM(pg[:], gv[:, kc, o4 * 128:(o4 + 1) * 128], h[:, kc, tk], kc == 0, kc == 7, gr + [Bh.r(kc, tile)], [rg])
                    th, Bth = pth_[cnt % 2]
                    tm, Btm = ptm[cnt % 2]
                    cnt += 1
                    ACT(th[:], pg[:], AF.Tanh, [rg], [Bth.r()], scale=0.5)
                    if which == 0:
                        STT("dve", merged[:, oc, tk], th[:], 1.0, po[:], ALU.add, ALU.mult, [Bth.r(), ro], [Bmg.r(oc, tile)])
                    else:
                        STT("dve", tm[:], th[:], 1.0, po[:], ALU.add, ALU.mult, [Bth.r(), ro], [Btm.r()])
                        if which == 1:
                            TT("dve", merged[:, oc, tk], merged[:, oc, tk], tm[:], ALU.add, [Bmg.r(oc, tile), Btm.r()], [Bmg.r(oc, tile)])
                        else:
                            TT("dve", mergedb[:, oc, tk], merged[:, oc, tk], tm[:], ALU.add, [Bmg.r(oc, tile), Btm.r()], [Bmgb.r(oc, tile)])

    def out_proj_residual(l, s, w_ap, kch, src, srcR, gcol):
        ncb = 4 if kch == 8 else 1
        nob = 8 // ncb
        for ob in range(nob):
            wi_, wv_, wr_ = load_w(w_ap[l, :, ob * ncb * 128:(ob + 1) * ncb * 128], kch, ncb * 128)
            for o4 in range(ncb):
                oc = ob * ncb + o4
                for tile in range(2):
                    tk = slice(tile * 512, (tile + 1) * 512)
                    xt = slice(s * T + tile * 512, s * T + (tile + 1) * 512)
                    po, ro = ps_mm()
                    for kc in range(kch):
                        MM(po[:], wv_[:, kc, o4 * 128:(o4 + 1) * 128], src[:, kc, tk], kc == 0, kc == kch - 1,
                           wr_ + [srcR(kc, tile)], [ro])
                    STT("dve", xres[:, oc, xt], po[:], gcol(oc), xres[:, oc, xt], ALU.mult, ALU.add,
                        [ro, Bx.r(s, oc, tile)] + RC, [Bx.r(s, oc, tile)])

    KT, BKT = region("KT", ZO, [128, 4352], BF16)
    VV, BVV = region("VV", ZO + 8704, [128, 34, 192], BF16)
    QT, BQT = region("QT", ZO + 21760, [128, 4, 1024], BF16)
    PTl = [region(f"PTl{i}", ZO + 29952 + i * 2048, [128, 1024], BF16) for i in range(2)]
    rdb = [region(f"rdb{i}", YO + 8192 + i * 2048, [128, 512], F32) for i in range(2)]
    cst, Bcst = region("cst", YO + 12288, [128, 2, 128], F32)
    candK, BcK = region("candK", ZO + 3072, [128, 2, 4, 128], BF16)
    candV, BcV = region("candV", ZO + 8704 + 4608, [128, 2, 4, 192], BF16)
    _pt = [0]

    def next_pt():
        i = _pt[0] % 2
        _pt[0] += 1
        return PTl[i]

    def q_proj(l, s, qcol, normcol, rope):
        qi = next_slot()
        qv = sv(qi, [128, 8, 512])
        q5 = sv(qi, [128, 8, 4, 2, 64])
        qr = []
        for half in range(2):
            for j in range(4):
                DMA("pool", qv[:, :, j * 128 + half * 64:j * 128 + half * 64 + 64],
                    w_in[l, :, qcol + half * 256 + j * 64:qcol + half * 256 + j * 64 + 64].rearrange("(k p) d -> p k d", p=128),
                    [], [Bslot[qi].r(half, j)])
                qr.append(Bslot[qi].r(half, j))
        for j in range(4):
            for tile in range(2):
                pa, ra = ps_mm()
                for kc in range(8):
                    MM(pa[:], qv[:, kc, j * 128:(j + 1) * 128], h[:, kc, tile * 512:(tile + 1) * 512], kc == 0, kc == 7, qr + [Bh.r(kc, tile)], [ra])
                qk_epilogue(pa, ra, QT[:, j, tile * 512:(tile + 1) * 512], [BQT.r(j, tile)], l, normcol, rope, tile)

    _fin = [0]

    def finalize(acc, racc, half, ncols, dst, dstR, escol, on_dve=False):
        rd, Brd = rdb[_fin[0] % 2]
        _fin[0] += 1
        dp = slice(64, 128) if half == 0 else slice(0, 64)
        npp = slice(0, 64) if half == 0 else slice(64, 128)
        if on_dve:
            S.add("dve", lambda e: e.reciprocal(out=rd[dp, 0:ncols], in_=acc[dp, 0:ncols]), reads=[racc], writes=[Brd.r()])
            TT("dve", dst, acc[npp, 0:ncols], rd[dp, 0:ncols], ALU.mult, [racc, Brd.r()], dstR)
            return
        if escol is not None:
            ACT(rd[dp, 0:ncols], acc[dp, 0:ncols], AF.Ln, [racc, Bes.r()], [Brd.r()], bias=escol[dp], scale=1.0)
        else:
            ACT(rd[dp, 0:ncols], acc[dp, 0:ncols], AF.Ln, [racc], [Brd.r()])
        ACT(rd[dp, 0:ncols], rd[dp, 0:ncols], AF.Exp, [Brd.r()], [Brd.r()], scale=-1.0)
        TT("dve", dst, acc[npp, 0:ncols], rd[dp, 0:ncols], ALU.mult, [racc, Brd.r()], dstR)

    def load_kv_local(Bd, src, kcol, vcol, kdst, vdst_blk0, nblk=8):
        DMA("sp", KT[:, kdst:kdst + 1024], src[:, kcol:kcol + 1024], [Bd.r()], [BKT.r("loc")])
        DMA("sp", VV[:, vdst_blk0:vdst_blk0 + nblk, :], src[:, vcol:vcol + nblk * 192].rearrange("p (b c) -> p b c", c=192),
            [Bd.r()], [BVV.r("loc")])

    def run_pipeline(steps):
        n = len(steps)
        if n == 0:
            return
        steps[0]["score"]()
        for i in range(n):
            steps[i]["exp"]()
            if i + 1 < n:
                steps[i + 1]["score"]()
            steps[i]["pv"]()
            if steps[i].get("fin"):
                steps[i]["fin"]()

    _sc = [0]
    _ac = [0]

    def next_score():
        i = _sc[0] % 2
        _sc[0] += 1
        return pb2[i], [Bbank[2 * i].r(), Bbank[2 * i + 1].r()], [bank[2 * i], bank[2 * i + 1]]

    def next_accpair():
        i = 2 + (_ac[0] % 2)
        _ac[0] += 1
        return [bank[2 * i], bank[2 * i + 1]], [Bbank[2 * i].r(), Bbank[2 * i + 1].r()]

    def attn_P(l, mixer, ooff):
        steps = []
        for tile_ in range(2):
          for j in range(4):
            grp = {}
            for seq in (2 * tile_, 2 * tile_ + 1):
                def mk(seq=seq, j=j, grp=grp):
                    st = {}
                    box = {}

                    def score():
                        box["sc"], box["rs"], box["bk"] = next_score()
                        for half in range(2):
                            hp = slice(half * 64, (half + 1) * 64)
                            for kb in range(2):
                                blk = seq * 2 + kb
                                MM(box["bk"][half][:, kb * 256:(kb + 1) * 256], KT[hp, blk * 128:(blk + 1) * 128],
                                   QT[hp, j, seq * 256:(seq + 1) * 256], True, True, [BKT.r("loc"), BQT.r(j, seq // 2)], [box["rs"][half]])

                    def exp():
                        box["pt"], box["Bpt"] = next_pt()
                        ACT(box["pt"][:, 0:1024], box["sc"][:, 0:1024], AF.Exp, box["rs"], [box["Bpt"].r()], scale=0.125)

                    def pv():
                        if seq % 2 == 0:
                            grp["accs"], grp["raccs"] = next_accpair()
                        accs, raccs = grp["accs"], grp["raccs"]
                        box["accs"], box["raccs"] = accs, raccs
                        for half in range(2):
                            for kb in range(2):
                                blk = seq * 2 + kb
                                MM(accs[half][:, (seq % 2) * 256:(seq % 2 + 1) * 256], VV[:, blk, half * 64:half * 64 + 128],
                                   box["pt"][:, half * 512 + kb * 256:half * 512 + (kb + 1) * 256], kb == 0, kb == 1,
                                   [BVV.r("loc"), box["Bpt"].r()], [raccs[half]])

                    def fin():
                        for half in range(2):
                            hp = slice(half * 64, (half + 1) * 64)
                            hd = j + 4 * half
                            finalize(box["accs"][half], box["raccs"][half], half, 512, obuf[hp, ooff + j, (seq // 2) * 512:(seq // 2 + 1) * 512],
                                     [Bo.r(ooff + j, seq // 2, "a", 0, half)],
                                     es[:, l * 8 + hd:l * 8 + hd + 1] if mixer == "c" else None)
                    st["score"], st["exp"], st["pv"], st["fin"] = score, exp, pv, (fin if seq % 2 == 1 else None)
                    return st
                steps.append(mk())
        run_pipeline(steps)

    def attn_SB(l):
        steps = []
        for j in range(4):
            for qt in range(2):
                grp = {}
                for kb in range(34):
                    def mk(j=j, qt=qt, kb=kb, grp=grp):
                        box = {}

                        def score():
                            box["sc"], box["rs"], box["bk"] = next_score()
                            for half in range(2):
                                hp = slice(half * 64, (half + 1) * 64)
                                MM(box["bk"][half], KT[hp, kb * 128:(kb + 1) * 128], QT[hp, j, qt * 512:(qt + 1) * 512], True, True,
                                   [BKT.r("loc"), BKT.r("ctx"), BQT.r(j, qt)], [box["rs"][half]])

                        def exp():
                            box["pt"], box["Bpt"] = next_pt()
                            ACT(box["pt"][:, 0:1024], box["sc"][:, 0:1024], AF.Exp, box["rs"], [box["Bpt"].r()], scale=0.125)

                        def pv():
                            if kb == 0:
                                grp["accs"], grp["raccs"] = next_accpair()
                            for half in range(2):
                                MM(grp["accs"][half], VV[:, kb, half * 64:half * 64 + 128], box["pt"][:, half * 512:(half + 1) * 512],
                                   kb == 0, kb == 33, [BVV.r("loc"), BVV.r("ctx"), box["Bpt"].r()], [grp["raccs"][half]])

                        def fin():
                            for half in range(2):
                                hp = slice(half * 64, (half + 1) * 64)
                                finalize(grp["accs"][half], grp["raccs"][half], half, 512, obuf[hp, j, qt * 512:(qt + 1) * 512],
                                         [Bo.r(j, qt, "a", 0, half)], None, on_dve=True)
                        return {"score": score, "exp": exp, "pv": pv, "fin": fin if kb == 33 else None}
                    steps.append(mk())
        run_pipeline(steps)

    def attn_SC(l, ooff):
        steps = []
        for j in range(4):
            grps = [{}, {}]
            for n in range(8):
                def mk(j=j, n=n, grp=grps[n // 4]):
                    box = {"grp": grp}
                    qs = slice(n * 128, (n + 1) * 128)
                    kcols = [0, 128, 256 + n * 128, 384 + n * 128, 512 + n * 128]
                    vblks = [0, 1, 2 + n, 3 + n, 4 + n]
                    mp = 2 if n == 0 else 0
                    mn = 3 if n == 7 else 1

                    def score():
                        box["sc"], box["rs"], box["bk"] = next_score()
                        for half in range(2):
                            hp = slice(half * 64, (half + 1) * 64)
                            for i5 in range(4):
                                MM(box["bk"][half][:, i5 * 128:(i5 + 1) * 128], KT[hp, kcols[i5]:kcols[i5] + 128], QT[hp, j, qs], True, True,
                                   [BKT.r("loc"), BKT.r("ctx"), BKT.r("halo"), BQT.r(j, n // 4)], [box["rs"][half]])

                    def exp():
                        box["pt"], box["Bpt"] = next_pt()
                        ACT(box["pt"][:, 0:1024], box["sc"][:, 0:1024], AF.Exp, box["rs"], [box["Bpt"].r()], scale=0.125)
                        for half in range(2):
                            TT("dve", box["pt"][:, half * 512 + 256:half * 512 + 384], box["pt"][:, half * 512 + 256:half * 512 + 384], maskb[:, mp, :], ALU.mult,
                               [box["Bpt"].r(), Bmask.r()], [box["Bpt"].r()])

                    def pv():
                        if n % 4 == 0:
                            grp["accs"], grp["raccs"] = next_accpair()
                        accs, raccs = grp["accs"], grp["raccs"]
                        box["accs"], box["raccs"] = accs, raccs
                        for half in range(2):
                            for i5 in range(4):
                                MM(accs[half][:, (n % 4) * 128:(n % 4 + 1) * 128], VV[:, vblks[i5], half * 64:half * 64 + 128],
                                   box["pt"][:, half * 512 + i5 * 128:half * 512 + (i5 + 1) * 128], i5 == 0, False,
                                   [BVV.r("loc"), BVV.r("ctx"), BVV.r("halo"), box["Bpt"].r()], [raccs[half]])
                    return {"score": score, "exp": exp, "pv": pv, "box": box, "kcols": kcols, "vblks": vblks, "mn": mn, "qs": qs, "j": j, "n": n}
                st = mk()
                def mk2(st=st, j=j, n=n):
                    box2 = {}
                    box = st["box"]
                    qs, kcols, vblks, mn = st["qs"], st["kcols"], st["vblks"], st["mn"]

                    def score():
                        box2["sc"], box2["rs"], box2["bk"] = next_score()
                        for half in range(2):
                            hp = slice(half * 64, (half + 1) * 64)
                            MM(box2["bk"][half][:, 0:128], KT[hp, kcols[4]:kcols[4] + 128], QT[hp, j, qs], True, True,
                               [BKT.r("loc"), BKT.r("ctx"), BKT.r("halo"), BQT.r(j, n // 4)], [box2["rs"][half]])

                    def exp():
                        box2["pt"], box2["Bpt"] = next_pt()
                        ACT(box2["pt"][:, 0:256].rearrange("p (a b) -> p a b", a=2),
                            box2["sc"][:, 0:1024].rearrange("p (a b) -> p a b", a=2)[:, :, 0:128], AF.Exp, box2["rs"], [box2["Bpt"].r()], scale=0.125)
                        for half in range(2):
                            TT("dve", box2["pt"][:, half * 128:(half + 1) * 128], box2["pt"][:, half * 128:(half + 1) * 128], maskb[:, mn, :], ALU.mult,
                               [box2["Bpt"].r(), Bmask.r()], [box2["Bpt"].r()])

                    def pv():
                        for half in range(2):
                            MM(box["accs"][half][:, (n % 4) * 128:(n % 4 + 1) * 128], VV[:, vblks[4], half * 64:half * 64 + 128],
                               box2["pt"][:, half * 128:(half + 1) * 128], False, True,
                               [BVV.r("loc"), BVV.r("ctx"), BVV.r("halo"), box2["Bpt"].r()], [box["raccs"][half]])

                    def fin():
                        for half in range(2):
                            hp = slice(half * 64, (half + 1) * 64)
                            hd = j + 4 * half
                            finalize(box["accs"][half], box["raccs"][half], half, 512, obuf[hp, ooff + j, (n // 4) * 512:(n // 4 + 1) * 512],
                                     [Bo.r(ooff + j, n // 4, "a", 0, half)], es[:, l * 8 + hd:l * 8 + hd + 1])
                    return {"score": score, "exp": exp, "pv": pv, "fin": fin if n % 4 == 3 else None}
                steps.append(st)
                steps.append(mk2())
        run_pipeline(steps)

    def load_ctx(l, kc_ap, vc_ap):
        DMA("sp", cst[:], kc_ap[l].rearrange("(b p) c -> p b c", p=128), [], [Bcst.r()])
        pa, ra = ps_mm()
        for b in range(2):
            TR(pa[:, b * 128:(b + 1) * 128], cst[:, b, :], [Bcst.r()], [ra])
        CP("act", KT[:, 0:256], pa[:, 0:256], [ra], [BKT.r("ctx")])
        DMA("sp", cst[:], vc_ap[l].rearrange("(b p) c -> p b c", p=128), [ra], [Bcst.r()])
        MSET("dve", VV[:, 0:2, 64:128], 1.0, [BVV.r("ctx")])
        CP("dve", VV[:, 0:2, :].rearrange("p b (a c) -> p b a c", c=64)[:, :, 0:3:2, :], cst[:].rearrange("p b (a c) -> p b a c", c=64),
           [Bcst.r()], [BVV.r("ctx")])

    def obufR_full(chunks):
        def f(kc, tile):
            return None
        return f

    actb, Bact = region("actb", YO, [128, 22, 1024], BF16)
    gpP = [at([128, 4, 258], F32, XO + 14336 + i * 4160) for i in range(2)]
    gpS = [at([128, 1, 1026], F32, XO + 14336 + i * 4160) for i in range(2)]
    Bgp = [S.buf(f"gp{i}", XO + 14336 + i * 4160, XO + 14336 + (i + 1) * 4160) for i in range(2)]
    cvbuf = [region(f"cv{i}", XO + 22656 + i * 4096, [128, 1024], F32) for i in range(2)]
    glb = [region(f"gl{i}", XO + 30848 + i * 1216, [128, 512], BF16) for i in range(2)]

    def ffn(l, s):
        nseq, L = (4, 256) if s == 0 else (1, 1024)
        fw = lambda k, j: PT[:, PRM["fcw"] + (l * 3 + k) * 22 + j:PRM["fcw"] + (l * 3 + k) * 22 + j + 1]
        fb = lambda j: PT[:, PRM["fcb"] + l * 22 + j:PRM["fcb"] + l * 22 + j + 1]
        for j in range(22):
            i = next_slot()
            v = sv(i, [128, 8, 256])
            DMA("pool", v[:, :, 0:128], w_up[l, :, j * 128:(j + 1) * 128].rearrange("(k p) n -> p k n", p=128), [], [Bslot[i].r(0)])
            DMA("pool", v[:, :, 128:256], w_up[l, :, 2816 + j * 128:2816 + (j + 1) * 128].rearrange("(k p) n -> p k n", p=128), [], [Bslot[i].r(1)])
            gp = (gpP if s == 0 else gpS)[j % 2]
            Bg = Bgp[j % 2]
            cv, Bcv = cvbuf[j % 2]
            MSET("dve", gp[:, :, 0:1], 0.0, [Bg.r("h")])
            MSET("dve", gp[:, :, L + 1:L + 2], 0.0, [Bg.r("h")])
            for tile in range(2):
                pa, ra = ps_mm()
                for kc in range(8):
                    MM(pa[:], v[:, kc, 0:128], h[:, kc, tile * 512:(tile + 1) * 512], kc == 0, kc == 7, [Bslot[i].r(0), Bh.r(kc, tile)], [ra])
                if s == 0:
                    CP("act", gp[:, tile * 2:tile * 2 + 2, 1:257], pa[:].rearrange("p (a b) -> p a b", a=2), [ra], [Bg.r(tile)])
                else:
                    CP("act", gp[:, 0, 1 + tile * 512:1 + (tile + 1) * 512], pa[:], [ra], [Bg.r(tile)])
            gR = [Bg.r(0), Bg.r(1), Bg.r("h")]
            cvv = cv[:].rearrange("p (a b) -> p a b", a=nseq)
            ACT(cvv, gp[:, :, 0:L], AF.Identity, gR + RPT, [Bcv.r()], bias=fb(j), scale=fw(0, j))
            STT("dve", cvv, gp[:, :, 1:L + 1], fw(1, j), cvv, ALU.mult, ALU.add, gR + RPT + [Bcv.r()], [Bcv.r()])
            STT("dve", cvv, gp[:, :, 2:L + 2], fw(2, j), cvv, ALU.mult, ALU.add, gR + RPT + [Bcv.r()], [Bcv.r()])
            if s == 1:
                CP("dve", eg[:, j, 0:2], gp[:, 0, 1:3], gR, [Beg.r(j)])
                CP("dve", eg[:, j, 2:4], gp[:, 0, 1023:1025], gR, [Beg.r(j)])
            for tile in range(2):
                gl, Bgl = glb[tile]
                ACT(gl[:], cv[:, tile * 512:(tile + 1) * 512], AF.Gelu_apprx_tanh, [Bcv.r()], [Bgl.r()])
                pv, rv = ps_mm()
                for kc in range(8):
                    MM(pv[:], v[:, kc, 128:256], h[:, kc, tile * 512:(tile + 1) * 512], kc == 0, kc == 7, [Bslot[i].r(1), Bh.r(kc, tile)], [rv])
                TT("dve", actb[:, j, tile * 512:(tile + 1) * 512], gl[:], pv[:], ALU.mult, [Bgl.r(), rv], [Bact.r(j, tile)])
                if s == 1:
                    col = 0 if tile == 0 else 511
                    CP("dve", ev[:, j, tile:tile + 1], pv[:, col:col + 1], [rv], [Bev.r(j)])
        if s == 1:
            egR = Beg.rs(range(22))
            CP("dve", ge[:, :, 0], eg[:, :, 0], egR, [Bge.r()])
            CP("dve", ge[:, :, 1], eg[:, :, 3], egR, [Bge.r()])
            Bdi, Bdo = S.buf(f"ccDi{l}"), S.buf(f"ccDo{l}")
            DMA("sp", ccD_in[l], ge[:].rearrange("p a b -> p (a b)"), [Bge.r()], [Bdi.r()])
            AG(ccD_in[l], ccD_out[l], [Bdi.r()], [Bdo.r()])
            DMA("sp", gg[:].rearrange("p r a b -> p r (a b)"), ccD_out[l].rearrange("(r p) n -> p r n", p=128), [Bdo.r()], [Bgg.r()])
            for (hh, so, e_) in ((0, 0, 1), (1, 4, 0)):
                TS("dve", ehal[:, hh, :], gg[:, 0, :, e_], sel[:, so:so + 1], ALU.mult, [Bgg.r(), Bsel.r()], [Behal.r()])
                for r_ in range(1, 4):
                    STT("dve", ehal[:, hh, :], gg[:, r_, :, e_], sel[:, so + r_:so + r_ + 1], ehal[:, hh, :], ALU.mult, ALU.add,
                        [Bgg.r(), Bsel.r(), Behal.r()], [Behal.r()])
            W0 = PT[:, PRM["fcw"] + (l * 3 + 0) * 22:PRM["fcw"] + (l * 3 + 0) * 22 + 22]
            W1 = PT[:, PRM["fcw"] + (l * 3 + 1) * 22:PRM["fcw"] + (l * 3 + 1) * 22 + 22]
            W2 = PT[:, PRM["fcw"] + (l * 3 + 2) * 22:PRM["fcw"] + (l * 3 + 2) * 22 + 22]
            FB = PT[:, PRM["fcb"] + l * 22:PRM["fcb"] + l * 22 + 22]
            evR = Bev.rs(range(22))
            for (e_, a0, a1, a2, tokc) in ((0, ehal[:, 0, :], eg[:, :, 0], eg[:, :, 1], 0), (1, eg[:, :, 2], eg[:, :, 3], ehal[:, 1, :], 1023)):
                t0 = etmp[:, e_ * 2, :]
                t1 = etmp[:, e_ * 2 + 1, :]
                RR = [Behal.r(), Betmp.r(e_)] + egR + RPT
                TT("dve", t0, a0, W0, ALU.mult, RR, [Betmp.r(e_)])
                TT("dve", t1, a1, W1, ALU.mult, RR, [Betmp.r(e_)])
                TT("dve", t0, t0, t1, ALU.add, RR, [Betmp.r(e_)])
                TT("dve", t1, a2, W2, ALU.mult, RR, [Betmp.r(e_)])
                TT("dve", t0, t0, t1, ALU.add, RR, [Betmp.r(e_)])
                TT("dve", t0, t0, FB, ALU.add, RR, [Betmp.r(e_)])
                ACT(t1, t0, AF.Gelu_apprx_tanh, [Betmp.r(e_)], [Betmp.r(e_)])
                TT("dve", actb[:, :, tokc], t1, ev[:, :, e_], ALU.mult, [Betmp.r(e_)] + evR + Bact.rs(range(22), [e_]), Bact.rs(range(22), [e_]))

    def sample_exchange_kv(l, Bd):
        Bao = [S.buf(f"ccAo{l}{m}") for m in range(2)]
        for m in range(2):
            AG(kvd[l][1][m], ccA_out[l][m], [Bd[m].r()], [Bao[m].r()])
        return Bao

    def sample_exchange_xa_issue(l):
        CP("dve", xe[:, :, 0:1], xaS[:, :, 0, 2:3], Bxa.rs(range(8), [0]), [Bxe.r()])
        CP("dve", xe[:, :, 1:3], xaS[:, :, 0, 1024:1026], Bxa.rs(range(8), [1]), [Bxe.r()])
        Bbi, Bbo = S.buf(f"ccBi{l}"), S.buf(f"ccBo{l}")
        DMA("sp", ccB_in[l], xe[:].rearrange("p a b -> p (a b)"), [Bxe.r()], [Bbi.r()])
        AG(ccB_in[l], ccB_out[l], [Bbi.r()], [Bbo.r()])
        return Bbo

    def sample_exchange_xa_finish(l, Bbo):
        DMA("sp", xg[:].rearrange("p r a b -> p r (a b)"), ccB_out[l].rearrange("(r p) n -> p r n", p=128), [Bbo.r()], [Bxg.r()])
        hR_ = Bxa.rs(range(8), ["h"])
        for (dst, so, src) in ((xaS[:, :, 0, 0:2], 0, lambda r_: xg[:, r_, :, 1:3]), (xaS[:, :, 0, 1026:1027], 4, lambda r_: xg[:, r_, :, 0:1])):
            TS("dve", dst, src(0), sel[:, so:so + 1], ALU.mult, [Bxg.r(), Bsel.r()], hR_)
            for r_ in range(1, 4):
                STT("dve", dst, src(r_), sel[:, so + r_:so + r_ + 1], dst, ALU.mult, ALU.add, [Bxg.r(), Bsel.r()] + hR_, hR_)

    def lru_exchange(l):
        allc = Bccc.rs(range(2), range(8), range(2))
        Bci, Bco = S.buf(f"ccCi{l}"), S.buf(f"ccCo{l}")
        DMA("sp", ccC_in[l], ccc[:].rearrange("p d n a -> p (d n a)"), allc, [Bci.r()])
        AG(ccC_in[l], ccC_out[l], [Bci.r()], [Bco.r()])
        return Bco

    def lru_exchange_finish(l, Bco):
        DMA("sp", ccg[:].rearrange("p r d n a -> p r (d n a)"), ccC_out[l].rearrange("(r p) n -> p r n", p=128), [Bco.r()], [Bccg.r()])
        stc = PRM["st"] + l * 16
        CP("dve", HH[:, 0, 0, :], PT[:, stc:stc + 8], RPT, [BHH.r()])
        for j in range(3):
            TT("dve", HH[:, 0, j + 1, :], HH[:, 0, j, :], ccg[:, j, 0, :, 0], ALU.mult, [BHH.r(), Bccg.r()], [BHH.r()])
            TT("dve", HH[:, 0, j + 1, :], HH[:, 0, j + 1, :], ccg[:, j, 0, :, 1], ALU.add, [BHH.r(), Bccg.r()], [BHH.r()])
        CP("dve", HH[:, 1, 3, :], PT[:, stc + 8:stc + 16], RPT, [BHH.r()])
        for j in (3, 2, 1):
            TT("dve", HH[:, 1, j - 1, :], HH[:, 1, j, :], ccg[:, j, 1, :, 0], ALU.mult, [BHH.r(), Bccg.r()], [BHH.r()])
            TT("dve", HH[:, 1, j - 1, :], HH[:, 1, j - 1, :], ccg[:, j, 1, :, 1], ALU.add, [BHH.r(), Bccg.r()], [BHH.r()])
        for d in range(2):
            TS("dve", hst[:, d, :], HH[:, d, 0, :], sel[:, 8:9], ALU.mult, [BHH.r(), Bsel.r()], [Bhst.r()])
            for j in range(1, 4):
                STT("dve", hst[:, d, :], HH[:, d, j, :], sel[:, 8 + j:9 + j], hst[:, d, :], ALU.mult, ALU.add, [BHH.r(), Bsel.r(), Bhst.r()], [Bhst.r()])

    def obR(chunk, tile):
        return Bo.r(chunk, tile)

    for l in range(DEPTH):
        for s in (1, 0):
            cidx = 1 if s == 1 else 0
            stage(f'L{l}s{s} start')
            rmsnorm_h(s, lambda kc: AA[:, l, s, 0, kc:kc + 1], lambda kc: mod[:, l, kc, cidx:cidx + 1])
            stage(f'L{l}s{s} T2 kv')
            Bxa.s = s
            xa = xaP if s == 0 else xaS
            L = 256 if s == 0 else 1024
            if s == 0:
                MSET("dve", xaP[:, :, :, 0:2], 0.0, Bxa.rs(range(8), ["h"]))
                MSET("dve", xaP[:, :, :, 258:259], 0.0, Bxa.rs(range(8), ["h"]))
            for b in range(2):
                xi, xv, xr = load_w(w_in[l, :, C_XA + b * 512:C_XA + (b + 1) * 512], 8, 512)
                for n4 in range(4):
                    n = b * 4 + n4
                    for tile in range(2):
                        pa, ra = ps_mm()
                        for kc in range(8):
                            MM(pa[:], xv[:, kc, n4 * 128:(n4 + 1) * 128], h[:, kc, tile * 512:(tile + 1) * 512], kc == 0, kc == 7,
                               xr + [Bh.r(kc, tile)], [ra])
                        if s == 0:
                            CP("act", xaP[:, n, tile * 2:tile * 2 + 2, 2:258], pa[:].rearrange("p (a b) -> p a b", a=2), [ra], [Bxa.r(n, tile)])
                        else:
                            CP("act", xaS[:, n, 0, 2 + tile * 512:2 + (tile + 1) * 512], pa[:], [ra], [Bxa.r(n, tile)])
            stage(f'L{l}s{s} T2 exch/LRU')
            if s == 1:
                Bbo_ = sample_exchange_xa_issue(l)
            Bd = kv_pass(l, s)
            if s == 1:
                Bao = sample_exchange_kv(l, Bd)
                sample_exchange_xa_finish(l, Bbo_)
                lru(l, s)
                Bco_ = lru_exchange(l)
                QS[0] = 1
                q_proj(l, s, C_QB, qkgc[:, l * 2:l * 2 + 1], True)
                QS[0] = 0
                lru_exchange_finish(l, Bco_)
                lru_fix(l)
            else:
                lru(l, s)
            stage(f'L{l}s{s} T4')
            branch_proj(l, 0, w_oa, 8, C_GA, list(range(8)))
            stage(f'L{l}s{s} T5')
            if s == 0:
                q_proj(l, s, C_QB, qkgc[:, l * 2:l * 2 + 1], False)
            if s == 0:
                load_kv_local(Bd[0], kvd[l][0][0], 0, 1024, 0, 0)
                attn_P(l, "b", 0)
            else:
                load_ctx(l, ckb, cvb)
                DMA("sp", KT[:, 256:4352].rearrange("p (r n) -> p r n", r=4), ccA_out[l][0][:, 0:1024].rearrange("(r p) n -> p r n", p=128),
                    [Bao[0].r()], [BKT.r("loc")])
                DMA("sp", VV[:, 2:34, :].rearrange("p (r b) c -> p r (b c)", r=4),
                    ccA_out[l][0][:, 1024:2560].rearrange("(r p) n -> p r n", p=128), [Bao[0].r()], [BVV.r("loc")])
                attn_SB(l)
            for j in range(4):
                for tile in range(2):
                    subs = [r_ for k_, r_ in Bo._res.items() if len(k_) == 5 and k_[0] == j and k_[1] == tile]
                    S.add("dve", lambda e: e.engine_nop(), reads=subs, writes=[Bo.r(j, tile)])
            branch_proj(l, 1, w_ob, 4, C_GB, [0, 1, 2, 3])
            stage(f'L{l}s{s} T7')
            q_proj(l, s, C_QC, None, s == 1)
            if s == 0:
                load_kv_local(Bd[1], kvd[l][0][1], 0, 1024, 0, 0)
                attn_P(l, "c", 0)
            else:
                load_ctx(l, ckc, cvc)
                DMA("sp", KT[:, 384:1408], kvd[l][1][1][:, 0:1024], [Bd[1].r()], [BKT.r("loc")])
                DMA("sp", VV[:, 3:11, :], kvd[l][1][1][:, 1024:2560].rearrange("p (b c) -> p b c", c=192), [Bd[1].r()], [BVV.r("loc")])
                for e_, blk in ((0, 7), (1, 0)):
                    DMA("sp", candK[:, e_, :, :], ccA_out[l][1][:, blk * 128:(blk + 1) * 128].rearrange("(r p) n -> p r n", p=128),
                        [Bao[1].r()], [BcK.r(e_)])
                    DMA("sp", candV[:, e_, :, :], ccA_out[l][1][:, 1024 + blk * 192:1024 + (blk + 1) * 192].rearrange("(r p) n -> p r n", p=128),
                        [Bao[1].r()], [BcV.r(e_)])
                for e_, so, kdst, vblk in ((0, 0, 256, 2), (1, 4, 1408, 11)):
                    TS("dve", KT[:, kdst:kdst + 128], candK[:, e_, 0, :], sel[:, so:so + 1], ALU.mult, [BcK.r(e_), Bsel.r()], [BKT.r("halo")])
                    TS("dve", VV[:, vblk, :], candV[:, e_, 0, :], sel[:, so:so + 1], ALU.mult, [BcV.r(e_), Bsel.r()], [BVV.r("halo")])
                    for r_ in range(1, 4):
                        STT("dve", KT[:, kdst:kdst + 128], candK[:, e_, r_, :], sel[:, so + r_:so + r_ + 1], KT[:, kdst:kdst + 128],
                            ALU.mult, ALU.add, [BcK.r(e_), Bsel.r(), BKT.r("halo")], [BKT.r("halo")])
                        STT("dve", VV[:, vblk, :], candV[:, e_, r_, :], sel[:, so + r_:so + r_ + 1], VV[:, vblk, :],
                            ALU.mult, ALU.add, [BcV.r(e_), Bsel.r(), BVV.r("halo")], [BVV.r("halo")])
                attn_SC(l, 0)
            for j in range(0, 4):
                for tile in range(2):
                    subs = [r_ for k_, r_ in Bo._res.items() if len(k_) == 5 and k_[0] == j and k_[1] == tile]
                    S.add("dve", lambda e: e.engine_nop(), reads=subs, writes=[Bo.r(j, tile)])
            branch_proj(l, 2, w_oc, 4, C_GC, [0, 1, 2, 3])
            stage(f'L{l}s{s} T9')
            out_proj_residual(l, s, w_out, 8, mergedb, lambda kc, tile: Bmgb.r(kc, tile), lambda oc: AA[:, l, s, 2, oc:oc + 1])
            stage(f'L{l}s{s} FFN')
            rmsnorm_h(s, lambda kc: AA[:, l, s, 1, kc:kc + 1], lambda kc: mod[:, l, 24 + kc, cidx:cidx + 1])
            ffn(l, s)
            out_proj_residual(l, s, w_down, 22, actb, lambda kc, tile: Bact.r(kc, tile), lambda oc: mod[:, l, 40 + oc, cidx:cidx + 1])
            if s == 0:
                pa, ra = ps_mm()
                TR(pa[0:64, 0:128], lst[:].rearrange("p a b c -> p (a b c)"), Blst.rs(range(4), range(2), range(8)), [ra])
                CP("dve", lstT[:], pa[0:64, 0:128], [ra], [BlstT.r()])
                DMA("sp", nlru[l], lstT[:], [BlstT.r()], [])

    stage('final')
    yb, Byb = region("yb", XO, [128, 8, 512], F32)
    yt = [region(f"yt{i}", XO + 16384 + i * 4096, [128, 1024], F32) for i in range(2)]
    fgc = lambda kc: PT[:, PRM["fg"] + kc:PRM["fg"] + kc + 1]
    cnt = 0
    for s in range(2):
        for tile in range(2):
            tok = slice(s * T + tile * 512, s * T + (tile + 1) * 512)
            xr = Bx.rs([s], range(8), [tile])
            ACT(nsq[:], xres[:, :, tok], AF.Square, xr, [Bnsq.r()])
            pa, ra = ps_mm()
            for kc in range(8):
                MM(pa[:], onesb[:], nsq[:, kc, :], kc == 0, kc == 7, [Bones.r(), Bnsq.r()], [ra])
            ACT(nrs[:], pa[:], AF.Sqrt, [ra, Beps.r()], [Bnrs.r()], bias=epsc[:], scale=1.0 / 1024)
            S.add("dve", lambda e: e.reciprocal(out=nrs[:], in_=nrs[:]), reads=[Bnrs.r()], writes=[Bnrs.r()])
            for kc in range(8):
                STT("dve", yb[:, kc, :], xres[:, kc, tok], fgc(kc), nrs[:], ALU.mult, ALU.mult, [Bx.r(s, kc, tile), Bnrs.r()] + RPT, [Byb.r(kc)])
            for t4 in range(4):
                yo, Byo = yt[cnt % 2]
                cnt += 1
                for g in range(2):
                    pa, ra = ps_mm()
                    for j in range(4):
                        kc = g * 4 + j
                        TR(pa[:, j * 128:(j + 1) * 128], yb[:, kc, t4 * 128:(t4 + 1) * 128], [Byb.r(kc)], [ra])
                    CP("act" if g else "dve", yo[:, g * 512:(g + 1) * 512], pa[:], [ra], [Byo.r(g)])
                row = s * T + tile * 512 + t4 * 128
                DMA("sp", y_out[row:row + 128, :], yo[:], [Byo.r(0), Byo.r(1)], [])

    with nc.allow_non_contiguous_dma(reason="small strided param / edge transfers"):
        S.emit()
    nc._sched = S
    return nc


def _rope_tables(core):
    r = core % 4
    t = np.arange(1024) + r * 1024
    row = (t // 64).astype(np.float32)
    col = (t % 64).astype(np.float32)
    d = np.arange(128) % 64
    half = d // 32
    jj = (d % 32) % 16
    inv = (10000.0 ** (-(jj.astype(np.float32)) / 16.0)).astype(np.float32)
    pos = np.where(half[:, None] == 0, row[None, :], col[None, :]).astype(np.float32)
    ang = (pos * inv[:, None]).astype(np.float32)
    return np.cos(ang).astype(np.float32), np.sin(ang).astype(np.float32)


def _rot_matrix():
    R = np.zeros((128, 128), np.float32)
    for m in range(128):
        if m % 32 < 16:
            R[m + 16, m] = -1.0
        else:
            R[m - 16, m] = 1.0
    return R


_NC_CACHE = {}
NCORES = 8
RAW = {}


def kernel(x_prompt, x_sample, c, cache_kb, cache_vb, cache_kc, cache_vc, state_lru,
           c_ctx, norm1_g, norm2_g, w_mod, b_mod, w_in, lru_conv_w, lru_conv_b,
           lru_wa, lru_ba, lru_wx, lru_bx, lru_lam, qnorm_g, knorm_g, sink_c,
           w_oa, w_ob, w_oc, w_out, w_up, ffn_conv_w, ffn_conv_b, w_down, final_g):
    f = lambda a: np.ascontiguousarray(np.asarray(a, dtype=np.float32))
    x_prompt, x_sample = f(x_prompt), f(x_sample)
    if "nc" not in _NC_CACHE:
        _NC_CACHE["nc"] = build()
    nc = _NC_CACHE["nc"]
    shared = {
        "w_mod": f(w_mod), "w_in": f(w_in), "lru_wa": f(lru_wa), "lru_wx": f(lru_wx), "w_oa": f(w_oa), "w_ob": f(w_ob),
        "w_oc": f(w_oc), "w_out": f(w_out), "w_up": f(w_up), "w_down": f(w_down),
        "qkg": np.ascontiguousarray(np.stack([f(qnorm_g), f(knorm_g)], axis=1)),
        "sinkc": f(sink_c), "identd": np.eye(128, dtype=np.float32), "rotd": _rot_matrix(),
    }
    jj, ii = np.meshgrid(np.arange(128), np.arange(128), indexing="ij")
    mprev = (jj >= ii).astype(np.float32)
    mnext = (jj <= ii).astype(np.float32)
    in_maps = []
    for core in range(8):
        g, r = core // 4, core % 4
        xin = np.concatenate([x_prompt[4 * core:4 * core + 4].reshape(1024, 1024), x_sample[g, r * 1024:(r + 1) * 1024]], axis=0)
        rows = [f(c_ctx).reshape(8, 128), f(c)[g].reshape(8, 128), f(norm1_g).reshape(16, 128), f(norm2_g).reshape(16, 128),
                f(b_mod).reshape(96, 128), f(lru_conv_w).reshape(64, 128), f(lru_conv_b).reshape(16, 128),
                f(lru_ba).reshape(32, 128), f(lru_bx).reshape(32, 128), f(lru_lam).reshape(32, 128),
                f(ffn_conv_w).reshape(132, 128), f(ffn_conv_b).reshape(44, 128), f(final_g).reshape(8, 128),
                f(state_lru)[g].reshape(32, 128)]
        prm = np.concatenate(rows, axis=0)
        prm = np.concatenate([prm, np.zeros((PRM_ROWS - prm.shape[0], 128), np.float32)], axis=0)
        cosd, sind = _rope_tables(core)
        selv = np.zeros((128, 16), np.float32)
        if r > 0:
            selv[:, r - 1] = 1.0
        if r < 3:
            selv[:, 4 + r + 1] = 1.0
        selv[:, 8 + r] = 1.0
        maskd = np.stack([mprev, mnext, mprev * (1.0 if r > 0 else 0.0), mnext * (1.0 if r < 3 else 0.0)], axis=0).astype(np.float32)
        m = dict(shared)
        m.update({
            "xin": np.ascontiguousarray(xin), "prm": np.ascontiguousarray(prm),
            "ckb": np.ascontiguousarray(f(cache_kb)[g].reshape(2, 256, 128)), "cvb": np.ascontiguousarray(f(cache_vb)[g].reshape(2, 256, 128)),
            "ckc": np.ascontiguousarray(f(cache_kc)[g].reshape(2, 256, 128)), "cvc": np.ascontiguousarray(f(cache_vc)[g].reshape(2, 256, 128)),
            "cosd": cosd, "sind": sind, "maskd": maskd, "seld": selv,
        })
        in_maps.append(m)
    res = run_bass_kernel_spmd(nc, in_maps[:NCORES], core_ids=list(range(NCORES)))
    R = res.results
    if NCORES != 8:
        RAW['R'] = R
        return None
    y_prompt = np.stack([R[cidx]["y_out"][0:1024].reshape(4, 256, 1024) for cidx in range(8)], 0).reshape(32, 256, 1024)
    y_sample = np.stack([np.concatenate([R[g * 4 + r]["y_out"][1024:2048] for r in range(4)], 0) for g in range(2)], 0)
    nkv = np.stack([R[cidx]["nkv"] for cidx in range(8)], 0)
    nkv = nkv.reshape(8, 2, 4, 256, 4, 2, 64).transpose(0, 2, 1, 3, 4, 5, 6).reshape(32, 2, 256, 4, 2, 64)
    new_kb, new_vb, new_kc, new_vc = (np.ascontiguousarray(nkv[:, :, :, i]) for i in range(4))
    nl = np.stack([R[cidx]["nlru"] for cidx in range(8)], 0)
    new_lru = nl.reshape(8, 2, 4, 2, 8, 128).transpose(0, 2, 1, 3, 4, 5).reshape(32, 2, 2, 1024)
    return (y_prompt.astype(np.float32), y_sample.astype(np.float32), new_kb, new_vb, new_kc, new_vc,
            np.ascontiguousarray(new_lru))
```

```python
import contextlib
import numpy as np
import ml_dtypes
import concourse.bass as bass
import concourse.mybir as mybir
from concourse.bass_utils import run_bass_kernel_spmd

F32 = mybir.dt.float32
BF16 = mybir.dt.bfloat16
ALU = mybir.AluOpType
AF = mybir.ActivationFunctionType

ENGS = ("pe", "act", "dve", "pool", "sp")
NDMASEM = 24
DEPTH = 2
T = 1024
EPS = 1e-6
GELU_C = 0.7978845608028654

C_XA, C_YA, C_QB, C_KB, C_VB, C_QC, C_KC, C_VC, C_GA, C_GB, C_GC = (
    0, 1024, 2048, 2560, 2688, 2816, 3328, 3456, 3584, 4608, 5632)

PRM = {}
_o = 0
for _n, _r in (("cond", 16), ("n1", 16), ("n2", 16), ("bmod", 96), ("lcw", 64), ("lcb", 16),
               ("ba", 32), ("bx", 32), ("lam", 32), ("fcw", 132), ("fcb", 44), ("fg", 8), ("st", 32)):
    PRM[_n] = _o
    _o += _r
PRM_ROWS = 640


class Res:
    __slots__ = ("buf", "w", "r")

    def __init__(self, buf):
        self.buf = buf
        self.w = None
        self.r = []


class Buf:
    def __init__(self, name, lo=None, hi=None, excl=False):
        self.name = name
        self.excl = excl
        self.lo = lo
        self.hi = hi
        self._res = {}
        self.ov = []
        self.acc = {}
        self.wr = {}
        self.gen = []

    def newgen(self):
        self.gen = [p for (_, p) in self.acc.values()]
        self._res = {}

    def r(self, *key):
        if key not in self._res:
            self._res[key] = Res(self)
        return self._res[key]

    def rs(self, *lists):
        out = []

        def rec(i, cur):
            if i == len(lists):
                out.append(self.r(*cur))
                return
            for v in lists[i]:
                rec(i + 1, cur + (v,))

        rec(0, ())
        return out


class Op:
    __slots__ = ("eng", "fn", "waits", "idx", "signal", "count", "dsem", "dval", "is_dma", "inc")


class Sched:
    def __init__(self, nc):
        self.nc = nc
        self.ops = {e: [] for e in ENGS}
        self.seen = {e: {} for e in ENGS}
        self.ndma = 0
        self.gcnt = {"sw": 0, "hw": 0, "cc": 0}
        self.dma_last = [None] * NDMASEM
        self.dma_cnt = [0] * NDMASEM
        self.bufs = []
        self.stopped = False

    def buf(self, name, lo=None, hi=None, excl=False):
        b = Buf(name, lo, hi, excl)
        if lo is not None:
            for o in self.bufs:
                if o.lo is not None and o.lo < hi and lo < o.hi:
                    o.ov.append(b)
                    b.ov.append(o)
        self.bufs.append(b)
        return b

    @staticmethod
    def _key(p):
        if p.is_dma:
            return ("d", p.dsem), p.dval
        return ("e", p.eng), p.idx

    def add(self, eng, fn, reads=(), writes=(), dma=False, inc=16):
        if self.stopped:
            return None
        op = Op()
        op.eng = eng
        op.fn = fn
        op.is_dma = dma
        op.signal = False
        op.count = None
        op.inc = inc
        op.idx = len(self.ops[eng])
        deps = []
        rb = set()
        wb = set()
        for r in reads:
            if r.w is not None:
                deps.append((r.w, True))
            if r.buf.excl:
                for rr in r.r:
                    deps.append((rr, False))
            rb.add(r.buf)
        for w in writes:
            if w.w is not None:
                deps.append((w.w, False))
            for rr in w.r:
                deps.append((rr, False))
            wb.add(w.buf)
        for b in rb | wb:
            for p in b.gen:
                deps.append((p, False))
        for b in rb:
            for o in b.ov:
                for (_, p) in o.wr.values():
                    deps.append((p, False))
        for b in wb:
            for o in b.ov:
                for (_, p) in o.acc.values():
                    deps.append((p, False))
        if dma:
            grp = "cc" if inc == 1 else ("sw" if eng == "pool" else "hw")
            lo_, n_ = {"sw": (0, 10), "hw": (10, 10), "cc": (20, 4)}[grp]
            s = lo_ + self.gcnt[grp] % n_
            self.gcnt[grp] += 1
            self.ndma += 1
            prev = self.dma_last[s]
            if prev is not None:
                deps.append((prev, True))
            self.dma_cnt[s] += inc
            op.dsem = s
            op.dval = self.dma_cnt[s]
            self.dma_last[s] = op
        waits = {}
        seen = self.seen[eng]
        for p, raw in deps:
            key, val = self._key(p)
            if not p.is_dma and p.eng == eng and eng == "pe":
                continue
            if seen.get(key, -1) >= val:
                continue
            if key not in waits or waits[key][0] < val:
                waits[key] = (val, p)
        for key, (val, p) in waits.items():
            seen[key] = val
            p.signal = True
        op.waits = [p for (_, p) in waits.values()]
        k, v = self._key(op)
        for r in reads:
            r.r.append(op)
        for w in writes:
            w.w = op
            w.r = []
        for b in rb | wb:
            if b.acc.get(k, (-1, None))[0] < v:
                b.acc[k] = (v, op)
        for b in wb:
            if b.wr.get(k, (-1, None))[0] < v:
                b.wr[k] = (v, op)
        self.ops[eng].append(op)
        return op

    def emit(self):
        nc = self.nc
        with contextlib.ExitStack() as st:
            esem = {e: st.enter_context(nc.semaphore(f"s_{e}")) for e in ENGS}
            dsem = [st.enter_context(nc.semaphore(f"s_dma{i}")) for i in range(NDMASEM)]
            for e in ENGS:
                c = 0
                for op in self.ops[e]:
                    if not op.is_dma and op.signal:
                        c += 1
                        op.count = c
            block = st.enter_context(nc.Block())

            def run(e, engobj):
                for op in self.ops[e]:
                    for p in op.waits:
                        if p.is_dma:
                            engobj.wait_ge(dsem[p.dsem], p.dval)
                        else:
                            engobj.wait_ge(esem[p.eng], p.count)
                    ins = op.fn(engobj)
                    if op.is_dma:
                        ins.then_inc(dsem[op.dsem], op.inc)
                    elif op.signal:
                        ins.then_inc(esem[e], 1)
                if e == "sp":
                    for s in range(NDMASEM):
                        if self.dma_cnt[s]:
                            engobj.wait_ge(dsem[s], self.dma_cnt[s])

            @block.tensor
            def _(eng):
                run("pe", eng)

            @block.scalar
            def _(eng):
                run("act", eng)

            @block.vector
            def _(eng):
                run("dve", eng)

            @block.gpsimd
            def _(eng):
                run("pool", eng)

            @block.sync
            def _(eng):
                run("sp", eng)


STOP = None
STAGES = []


def build(debug=False):
    nc = bass.Bass("TRN2", target_bir_lowering=False)
    S = Sched(nc)
    _stg = [0]

    def stage(name):
        _stg[0] += 1
        STAGES.append((name, len(S.ops["pe"])))
        hit = (STOP is not None) and ((_stg[0] >= STOP) if isinstance(STOP, int) else (name == STOP))
        if hit and not S.stopped:
            print("STOP at stage", _stg[0], name, flush=True)
            S.stopped = True

    def din(name, shape, dt=F32):
        return nc.dram_tensor(name, list(shape), dt, kind="ExternalInput").ap()

    def dout(name, shape, dt=F32):
        return nc.dram_tensor(name, list(shape), dt, kind="ExternalOutput").ap()

    def dint(name, shape, dt=F32):
        return nc.dram_tensor(name, list(shape), dt).ap()

    xin = din("xin", [2048, 1024])
    prm = din("prm", [PRM_ROWS, 128])
    w_mod = din("w_mod", [2, 1024, 6144])
    w_in = din("w_in", [2, 1024, 6656])
    lru_wa = din("lru_wa", [2, 2, 8, 128, 128])
    lru_wx = din("lru_wx", [2, 2, 8, 128, 128])
    w_oa = din("w_oa", [2, 1024, 1024])
    w_ob = din("w_ob", [2, 512, 1024])
    w_oc = din("w_oc", [2, 512, 1024])
    w_out = din("w_out", [2, 1024, 1024])
    w_up = din("w_up", [2, 1024, 5632])
    w_down = din("w_down", [2, 2816, 1024])
    qkg = din("qkg", [2, 2, 64])
    sinkc = din("sinkc", [2, 8])
    ckb = din("ckb", [2, 256, 128])
    cvb = din("cvb", [2, 256, 128])
    ckc = din("ckc", [2, 256, 128])
    cvc = din("cvc", [2, 256, 128])
    identd = din("identd", [128, 128])
    rotd = din("rotd", [128, 128])
    cosd = din("cosd", [128, 1024])
    sind = din("sind", [128, 1024])
    maskd = din("maskd", [4, 128, 128])
    seld = din("seld", [128, 16])

    y_out = dout("y_out", [2048, 1024])
    nkv = dout("nkv", [2, 1024, 512])
    nlru = dout("nlru", [2, 64, 128])

    kvd = [[[dint(f"kvd{l}_{s}_{m}", [128, 2560], BF16) for m in range(2)] for s in range(2)] for l in range(2)]
    ccA_out = [[dint(f"ccAo{l}_{m}", [512, 2560], BF16) for m in range(2)] for l in range(2)]
    ccB_in = [dint(f"ccBi{l}", [128, 24]) for l in range(2)]
    ccB_out = [dint(f"ccBo{l}", [512, 24]) for l in range(2)]
    ccC_in = [dint(f"ccCi{l}", [128, 32]) for l in range(2)]
    ccC_out = [dint(f"ccCo{l}", [512, 32]) for l in range(2)]
    ccD_in = [dint(f"ccDi{l}", [128, 44]) for l in range(2)]
    ccD_out = [dint(f"ccDo{l}", [512, 44]) for l in range(2)]
    RG = [[0, 1, 2, 3], [4, 5, 6, 7]]

    ARENA = 212480
    arena = nc.alloc_sbuf_tensor("arena", [128, ARENA // 4], F32)
    base = nc.lookup_mloc(arena).addr
    _cnt = [0]

    def at(shape, dt, off):
        _cnt[0] += 1
        return nc.alloc_sbuf_tensor_at(f"t{_cnt[0]}", list(shape), dt, offset=base + off)

    def nbytes(shape, dt):
        n = 1
        for s_ in shape[1:]:
            n *= s_
        return n * (2 if dt == BF16 else 4)

    _bump = [0]

    def fixed(name, shape, dt):
        off = _bump[0]
        sz = (nbytes(shape, dt) + 31) // 32 * 32
        _bump[0] += sz
        return at(shape, dt, off), S.buf(name, off, off + sz)

    def region(name, off, shape, dt):
        return at(shape, dt, off), S.buf(name, off, off + nbytes(shape, dt))

    xres, Bx = fixed("xres", [128, 8, 2048], F32)
    h, Bh = fixed("h", [128, 8, 1024], BF16)
    NSLOT = 3
    slot_off = []
    Bslot = []
    for i in range(NSLOT):
        slot_off.append(_bump[0])
        _, b = fixed(f"slot{i}", [128, 4096], BF16)
        Bslot.append(b)
    ident, Bid = fixed("ident", [128, 128], F32)
    rotb, Brot = fixed("rotb", [128, 128], BF16)
    onesb, Bones = fixed("onesb", [128, 128], BF16)
    onesblk, Boblk = fixed("onesblk", [128, 128], BF16)
    maskb, Bmask = fixed("maskb", [128, 4, 128], BF16)
    cosb, Bcos = fixed("cosb", [128, 1024], F32)
    sinb, Bsin = fixed("sinb", [128, 1024], F32)
    PT, BPT = fixed("PT", [128, PRM_ROWS], F32)
    mod, Bmod = fixed("mod", [128, 2, 48, 2], F32)
    AA, BAA = fixed("AA", [128, 2, 2, 3, 8], F32)
    cn, Bcn = fixed("cn", [128, 2, 32], F32)
    hbias, Bhb = fixed("hbias", [128, 2, 32], F32)
    sel, Bsel = fixed("sel", [128, 16], F32)
    es, Bes = fixed("es", [128, 16], F32)
    qkgc, Bqkg = fixed("qkgc", [128, 4], F32)
    kngbc, Bkng = fixed("kngbc", [128, 2, 64], F32)
    sc, Bsc = fixed("sc", [128, 8, 2], BF16)
    scf, Bscf = fixed("scf", [128, 8, 2], F32)
    lst, Blst = fixed("lst", [128, 4, 2, 8], F32)
    lstT, BlstT = fixed("lstT", [64, 128], F32)
    hst, Bhst = fixed("hst", [128, 2, 8], F32)
    smr, Bsmr = fixed("smr", [128, 2, 8, 2], F32)
    ccc, Bccc = fixed("ccc", [128, 2, 8, 2], F32)
    ccg, Bccg = fixed("ccg", [128, 4, 2, 8, 2], F32)
    carry, Bcar = fixed("carry", [128, 2, 8], F32)
    carryP, BcarP = fixed("carryP", [128, 2, 8], F32)
    HH, BHH = fixed("HH", [128, 2, 5, 8], F32)
    xe, Bxe = fixed("xe", [128, 8, 3], F32)
    xg, Bxg = fixed("xg", [128, 4, 8, 3], F32)
    eg, Beg = fixed("eg", [128, 22, 4], F32)
    ev, Bev = fixed("ev", [128, 22, 2], F32)
    ge, Bge = fixed("ge", [128, 22, 2], F32)
    gg, Bgg = fixed("gg", [128, 4, 22, 2], F32)
    ehal, Behal = fixed("ehal", [128, 2, 22], F32)
    etmp, Betmp = fixed("etmp", [128, 4, 22], F32)
    epsc, Beps = fixed("epsc", [128, 1], F32)
    junk, Bjunk = fixed("junk", [128, 64], F32)
    ssk, Bssk = fixed("ssk", [128, 4], F32)
    assert _bump[0] <= 126976, _bump[0]
    XO, YO, ZO = 126976, 160256, 176640
    ZEND = ARENA

    def slot_view(i, shape):
        return at(shape, BF16, slot_off[i])

    slotv = {}

    def sv(i, shape):
        k = (i, tuple(shape))
        if k not in slotv:
            slotv[k] = slot_view(i, shape)
        return slotv[k]

    _slot_rr = [0]

    def next_slot():
        i = _slot_rr[0] % NSLOT
        _slot_rr[0] += 1
        Bslot[i].newgen()
        return i

    pb2 = [nc.alloc_psum_tensor(f"pb{i}", [128, 1024], F32) for i in range(4)]
    bank = [pb2[b // 2][:, (b % 2) * 512:(b % 2 + 1) * 512] for b in range(8)]
    Bbank = [S.buf(f"bank{b}", excl=True) for b in range(8)]
    pmm = bank[0:4]
    Bpmm = Bbank[0:4]
    pacc = bank[4:6]
    Bpacc = Bbank[4:6]
    pbig = pb2[3]
    _rr = {"mm": 0, "acc": 0}

    def ps_mm():
        i = _rr["mm"] % 8
        _rr["mm"] += 1
        return bank[i], Bbank[i].r()

    def ps_acc():
        i = _rr["acc"] % 2
        _rr["acc"] += 1
        return pacc[i], Bpacc[i].r()

    def ACT(out, in_, func, R, W, bias=0.0, scale=1.0, accum=None):
        kw = {}
        if accum is not None:
            kw["accum_out"] = accum
        S.add("act", lambda e: e.activation(out=out, in_=in_, func=func, bias=bias, scale=scale, **kw), reads=R, writes=W)

    def TT(eng, out, a, b, op, R, W):
        S.add(eng, lambda e: e.tensor_tensor(out=out, in0=a, in1=b, op=op), reads=R, writes=W)

    def TS(eng, out, a, s1, op0, R, W, s2=None, op1=None):
        if op1 is None:
            S.add(eng, lambda e: e.tensor_scalar(out=out, in0=a, scalar1=s1, scalar2=None, op0=op0), reads=R, writes=W)
        else:
            S.add(eng, lambda e: e.tensor_scalar(out=out, in0=a, scalar1=s1, scalar2=s2, op0=op0, op1=op1), reads=R, writes=W)

    def STT(eng, out, a, scal, b, op0, op1, R, W):
        S.add(eng, lambda e: e.scalar_tensor_tensor(out=out, in0=a, scalar=scal, in1=b, op0=op0, op1=op1), reads=R, writes=W)

    def CP(eng, out, in_, R, W):
        if eng == "act":
            S.add("act", lambda e: e.copy(out=out, in_=in_), reads=R, writes=W)
        else:
            S.add(eng, lambda e: e.tensor_copy(out=out, in_=in_), reads=R, writes=W)

    def MSET(eng, ap, val, W):
        S.add(eng, lambda e: e.memset(ap, val), writes=W)

    def MM(out, lhsT, rhs, start, stop, R, W):
        S.add("pe", lambda e: e.matmul(out, lhsT, rhs, start=start, stop=stop), reads=R, writes=W)

    def TR(out, in_, R, W):
        S.add("pe", lambda e: e.transpose(out, in_, ident[:]), reads=R + [Bid.r()], writes=W)

    def DMA(q, out, in_, R, W):
        S.add(q, lambda e: e.dma_start(out=out, in_=in_), reads=R, writes=W, dma=True)

    def AG(in_ap, out_ap, R, W):
        S.add("pool", lambda e: e.collective_compute("AllGather", ALU.bypass, replica_groups=RG, ins=[in_ap], outs=[out_ap]),
              reads=R, writes=W, dma=True, inc=1)

    def load_w(src, kch, ncols, parts=1):
        i = next_slot()
        v = sv(i, [128, kch, ncols])
        DMA("pool", v[:], src.rearrange("(k p) n -> p k n", p=128), [], [Bslot[i].r(0)])
        return i, v, [Bslot[i].r(0)]

    DMA("sp", ident[:], identd, [], [Bid.r()])
    DMA("pool", rotb[:], rotd, [], [Brot.r()])
    DMA("pool", maskb[:], maskd.rearrange("m p n -> p m n"), [], [Bmask.r()])
    DMA("sp", cosb[:], cosd, [], [Bcos.r()])
    DMA("sp", sinb[:], sind, [], [Bsin.r()])
    DMA("sp", sel[:], seld, [], [Bsel.r()])
    MSET("dve", onesb[:], 1.0, [Bones.r()])
    MSET("dve", onesblk[:], 0.0, [Boblk.r()])
    MSET("dve", onesblk[0:64, 0:64], 1.0, [Boblk.r()])
    MSET("dve", onesblk[64:128, 64:128], 1.0, [Boblk.r()])
    MSET("dve", epsc[:], EPS, [Beps.r()])
    with nc.allow_non_contiguous_dma(reason="tiny param loads"):
        for l in range(2):
            for qk in range(2):
                for half in range(2):
                    src = bass.AP(qkg.tensor, (l * 2 + qk) * 64, [[1, 64], [1, 1]])
                    DMA("sp", qkgc[half * 64:(half + 1) * 64, l * 2 + qk:l * 2 + qk + 1], src, [], [Bqkg.r(l, qk, half)])
            src = bass.AP(qkg.tensor, (l * 2 + 1) * 64, [[0, 128], [1, 64]])
            DMA("sp", kngbc[:, l, :], src, [], [Bkng.r(l)])
        src = bass.AP(sinkc.tensor, 0, [[0, 128], [1, 16]])
        DMA("sp", es[:], src, [], [Bes.r()])
    ACT(es[:], es[:], AF.Exp, [Bes.r()], [Bes.r()])

    pstage, Bpst = region("pstage", ZO, [128, 5, 128], F32)
    DMA("sp", pstage[:], prm.rearrange("(g r) c -> r g c", r=128), [], [Bpst.r()])
    pa, ra = ps_mm()
    for g in range(4):
        TR(pa[:, g * 128:(g + 1) * 128], pstage[:, g, :], [Bpst.r()], [ra])
    CP("dve", PT[:, 0:512], pa[:, 0:512], [ra], [BPT.r()])
    pa, ra = ps_mm()
    TR(pa[:, 0:128], pstage[:, 4, :], [Bpst.r()], [ra])
    CP("dve", PT[:, 512:640], pa[:, 0:128], [ra], [BPT.r()])
    RPT = [BPT.r()]

    for c in range(2):
        ACT(sc[:, :, c], PT[:, PRM["cond"] + c * 8:PRM["cond"] + c * 8 + 8], AF.Silu, RPT, [Bsc.r()])
        ACT(scf[:, :, c], PT[:, PRM["cond"] + c * 8:PRM["cond"] + c * 8 + 8], AF.Silu, RPT, [Bscf.r()])
    lamv = PT[:, PRM["lam"]:PRM["lam"] + 32]
    ACT(cn[:, 0, :], lamv, AF.Exp, RPT, [Bcn.r()], scale=-1.0)
    ACT(cn[:, 0, :], cn[:, 0, :], AF.Ln, [Bcn.r()], [Bcn.r()], bias=1.0)
    TS("dve", cn[:, 1, :], cn[:, 0, :], -8.0, ALU.mult, [Bcn.r()], [Bcn.r()])
    TS("dve", cn[:, 0, :], cn[:, 0, :], -4.0, ALU.mult, [Bcn.r()], [Bcn.r()])
    TS("dve", hbias[:, 0, :], PT[:, PRM["ba"]:PRM["ba"] + 32], 0.5, ALU.mult, RPT, [Bhb.r()])
    TS("dve", hbias[:, 1, :], PT[:, PRM["bx"]:PRM["bx"] + 32], 0.5, ALU.mult, RPT, [Bhb.r()])

    stage('adaln')
    wf32 = [region(f"wf32_{i}", XO + i * 16384, [128, 8, 512], F32) for i in range(2)]
    for l in range(2):
        pm, rm = ps_mm()
        for blk in range(12):
            if blk % 2 == 0:
                i, v, rw = load_w(w_mod[l, :, blk * 512:(blk + 1) * 512], 8, 512)
                rhs_, rhsR = sc, Bsc.r()
            else:
                v, Bwf = wf32[(blk // 2) % 2]
                DMA("sp", v[:], w_mod[l, :, blk * 512:(blk + 1) * 512].rearrange("(k p) n -> p k n", p=128), [], [Bwf.r()])
                rw = [Bwf.r()]
                rhs_, rhsR = scf, Bscf.r()
            for fc in range(4):
                f = blk * 4 + fc
                for kc in range(8):
                    MM(pm[:, f * 2:f * 2 + 2], v[:, kc, fc * 128:(fc + 1) * 128], rhs_[:, kc, :], kc == 0, kc == 7,
                       rw + [rhsR], [rm])
        for c in range(2):
            TT("dve", mod[:, l, :, c], pm[:, c:96:2], PT[:, PRM["bmod"] + l * 48:PRM["bmod"] + l * 48 + 48], ALU.add,
               [rm] + RPT, [Bmod.r()])
        for s in range(2):
            c = 0 if s == 0 else 1
            STT("dve", AA[:, l, s, 0, :], mod[:, l, 8:16, c], 1.0, PT[:, PRM["n1"] + l * 8:PRM["n1"] + l * 8 + 8],
                ALU.add, ALU.mult, [Bmod.r()] + RPT, [BAA.r()])
            STT("dve", AA[:, l, s, 1, :], mod[:, l, 32:40, c], 1.0, PT[:, PRM["n2"] + l * 8:PRM["n2"] + l * 8 + 8],
                ALU.add, ALU.mult, [Bmod.r()] + RPT, [BAA.r()])
            TS("dve", AA[:, l, s, 2, :], mod[:, l, 16:24, c], 0.5, ALU.mult, [Bmod.r()], [BAA.r()])
    RC = [BAA.r(), Bmod.r()] + RPT

    stage('xload')
    xst = [region(f"xst{i}", ZO + 4096 + i * 4096, [128, 1024], F32) for i in range(2)]
    for tt in range(16):
        xs_, Bxs = xst[tt % 2]
        DMA("sp", xs_[:], xin[tt * 128:(tt + 1) * 128, :], [], [Bxs.r()])
        s, tile = tt // 8, (tt % 8) // 4
        for g in range(2):
            pa, ra = ps_mm()
            for j in range(4):
                kc = g * 4 + j
                TR(pa[:, j * 128:(j + 1) * 128], xs_[:, kc * 128:(kc + 1) * 128], [Bxs.r()], [ra])
            eng = "dve" if g == 0 else "act"
            CP(eng, xres[:, g * 4:g * 4 + 4, tt * 128:(tt + 1) * 128], pa[:, 0:512].rearrange("p (k n) -> p k n", k=4),
               [ra], Bx.rs([s], range(g * 4, g * 4 + 4), [tile]))

    nsq, Bnsq = region("nsq", ZO, [128, 8, 512], BF16)
    nrs, Bnrs = region("nrs", ZO + 8192, [128, 512], F32)
    ntm = [region(f"ntm{i}", ZO + 10240 + i * 2048, [128, 512], F32) for i in range(2)]

    def rmsnorm_h(s, acol, bcol):
        for tile in range(2):
            tok = slice(s * T + tile * 512, s * T + (tile + 1) * 512)
            xr = Bx.rs([s], range(8), [tile])
            ACT(nsq[:], xres[:, :, tok], AF.Square, xr, [Bnsq.r()])
            pa, ra = ps_mm()
            for kc in range(8):
                MM(pa[:], onesb[:], nsq[:, kc, :], kc == 0, kc == 7, [Bones.r(), Bnsq.r()], [ra])
            ACT(nrs[:], pa[:], AF.Sqrt, [ra, Beps.r()], [Bnrs.r()], bias=epsc[:], scale=1.0 / 1024)
            S.add("dve", lambda e: e.reciprocal(out=nrs[:], in_=nrs[:]), reads=[Bnrs.r()], writes=[Bnrs.r()])
            for kc in range(8):
                tm, Btm = ntm[kc % 2]
                TT("dve", tm[:], xres[:, kc, tok], nrs[:], ALU.mult, [Bx.r(s, kc, tile), Bnrs.r()], [Btm.r()])
                ACT(h[:, kc, tile * 512:(tile + 1) * 512], tm[:], AF.Identity, [Btm.r()] + RC, [Bh.r(kc, tile)],
                    bias=bcol(kc), scale=acol(kc))

    def hR(tile):
        return Bh.rs(range(8), [tile])

    qsq, Bqsq = region("qsq", YO + 8192, [128, 512], BF16)
    qrs, Bqrs = region("qrs", YO + 9216, [128, 512], F32)
    qn, Bqn = region("qn", YO + 11264, [128, 512], F32)
    qnb, Bqnb = region("qnb", YO + 13312, [128, 512], BF16)
    qt1, Bqt1 = region("qt1", YO + 14336, [128, 512], F32)

    scr_sets = [((qsq, Bqsq), (qrs, Bqrs), (qn, Bqn), (qnb, Bqnb), (qt1, Bqt1)),
                (region("qsqZ", ZO, [128, 512], BF16), region("qrsZ", ZO + 1024, [128, 512], F32), region("qnZ", ZO + 3072, [128, 512], F32),
                 region("qnbZ", ZO + 5120, [128, 512], BF16), region("qt1Z", ZO + 6144, [128, 512], F32))]
    QS = [0]

    def qk_epilogue(pa, ra, dst, dstR, l, normcol, rope, tile):
        (qsq, Bqsq), (qrs, Bqrs), (qn, Bqn), (qnb, Bqnb), (qt1, Bqt1) = scr_sets[QS[0]]
        tk = slice(tile * 512, (tile + 1) * 512)
        if normcol is not None:
            ACT(qsq[:], pa[:], AF.Square, [ra], [Bqsq.r()])
            pb, rb = ps_mm()
            MM(pb[:], onesblk[:], qsq[:], True, True, [Boblk.r(), Bqsq.r()], [rb])
            ACT(qrs[:], pb[:], AF.Sqrt, [rb, Beps.r()], [Bqrs.r()], bias=epsc[:], scale=1.0 / 64)
            S.add("dve", lambda e: e.reciprocal(out=qrs[:], in_=qrs[:]), reads=[Bqrs.r()], writes=[Bqrs.r()])
            if not rope:
                STT("dve", dst, pa[:], normcol, qrs[:], ALU.mult, ALU.mult, [ra, Bqrs.r(), Bqkg.r(l, 0, 0), Bqkg.r(l, 0, 1), Bqkg.r(l, 1, 0), Bqkg.r(l, 1, 1)], dstR)
                return
            STT("dve", qn[:], pa[:], normcol, qrs[:], ALU.mult, ALU.mult, [ra, Bqrs.r(), Bqkg.r(l, 0, 0), Bqkg.r(l, 0, 1), Bqkg.r(l, 1, 0), Bqkg.r(l, 1, 1)], [Bqn.r()])
            src, srcR = qn[:], Bqn.r()
        else:
            if not rope:
                CP("act", dst, pa[:], [ra], dstR)
                return
            src, srcR = pa[:], ra
        stage('rope0')
        CP("act", qnb[:], src, [srcR], [Bqnb.r()])
        stage('rope1')
        pb, rb = ps_mm()
        MM(pb[:], rotb[:], qnb[:], True, True, [Brot.r(), Bqnb.r()], [rb])
        stage('rope2')
        TT("dve", qt1[:], src, cosb[:, tk], ALU.mult, [srcR, Bcos.r()], [Bqt1.r()])
        stage('rope3')
        TT("dve", qn[:], pb[:], sinb[:, tk], ALU.mult, [rb, Bsin.r()], [Bqn.r()])
        stage('rope4')
        TT("dve", dst, qt1[:], qn[:], ALU.add, [Bqt1.r(), Bqn.r()], dstR)
        stage('rope5')

    kvst, Bkvst = region("kvst", ZO + 16384, [128, 5120], BF16)
    kvout = [region(f"kvout{i}", ZO + 26624 + i * 2048, [128, 512], F32) for i in range(2)]
    xaP = at([128, 8, 4, 259], F32, XO)
    xaS = at([128, 8, 1, 1027], F32, XO)
    xaG = at([128, 8, 2054], BF16, XO)
    BxaPl = [S.buf(f"xaP{n}", XO + n * 4144, XO + (n + 1) * 4144) for n in range(8)]
    BxaSl = [S.buf(f"xaS{n}", XO + n * 4108, XO + (n + 1) * 4108) for n in range(8)]
    BxaGl = [S.buf(f"xaG{n}", XO + n * 4108, XO + (n + 1) * 4108) for n in range(8)]

    class _XaR:
        def __init__(self):
            self.s = 0

        def r(self, n, k):
            return (BxaPl if self.s == 0 else BxaSl)[n].r(k)

        def rs(self, ns, ks):
            return [self.r(n, k) for n in ns for k in ks]
    Bxa = _XaR()

    def kv_pass(l, s):
        i = next_slot()
        v = sv(i, [128, 8, 512])
        DMA("pool", v[:, :, 0:256], w_in[l, :, C_KB:C_KB + 256].rearrange("(k p) n -> p k n", p=128), [], [Bslot[i].r(0)])
        DMA("pool", v[:, :, 256:512], w_in[l, :, C_KC:C_KC + 256].rearrange("(k p) n -> p k n", p=128), [], [Bslot[i].r(1)])
        rw = [Bslot[i].r(0), Bslot[i].r(1)]
        vvb = kvst[:, 1024:2560].rearrange("p (b c) -> p b c", c=192)
        vvc = kvst[:, 3584:5120].rearrange("p (b c) -> p b c", c=192)
        MSET("dve", vvb[:, :, 64:128], 1.0, [Bkvst.r("ob")])
        MSET("dve", vvc[:, :, 64:128], 1.0, [Bkvst.r("oc")])
        stage('kv_a')
        for tt in range(8):
            tile = tt // 4
            pa, ra = ps_mm()
            for kc in range(8):
                MM(pa[:], h[:, kc, tt * 128:(tt + 1) * 128], v[:, kc, :], kc == 0, kc == 7, rw + [Bh.r(kc, tile)], [ra])
            CP("dve", vvb[:, tt, :].rearrange("p (a c) -> p a c", c=64)[:, 0:3:2, :],
               pa[:, 128:256].rearrange("p (a c) -> p a c", c=64), [ra], [Bkvst.r("vb", tt)])
            CP("dve", vvc[:, tt, :].rearrange("p (a c) -> p a c", c=64)[:, 0:3:2, :],
               pa[:, 384:512].rearrange("p (a c) -> p a c", c=64), [ra], [Bkvst.r("vc", tt)])
            if s == 0:
                ko, Bko = kvout[tt % 2]
                MSET("dve", ssk[:, 0:2], 0.0, [Bssk.r(0), Bssk.r(1)])
                for hd in range(2):
                    ACT(junk[:], pa[:, hd * 64:(hd + 1) * 64], AF.Square, [ra, Bssk.r(hd)], [Bjunk.r(), Bssk.r(hd)], accum=ssk[:, hd:hd + 1])
                ACT(ssk[:, 2:4], ssk[:, 0:2], AF.Sqrt, [Bssk.r(0), Bssk.r(1), Beps.r()], [Bssk.r(2)], bias=epsc[:], scale=1.0 / 64)
                S.add("dve", lambda e: e.reciprocal(out=ssk[:, 2:4], in_=ssk[:, 2:4]), reads=[Bssk.r(2)], writes=[Bssk.r(2)])
                for hd in range(2):
                    STT("dve", ko[:, hd * 64:(hd + 1) * 64], pa[:, hd * 64:(hd + 1) * 64], ssk[:, 2 + hd:3 + hd], kngbc[:, l, :],
                        ALU.mult, ALU.mult, [ra, Bssk.r(2), Bkng.r(l)], [Bko.r()])
                CP("act", ko[:, 128:512], pa[:, 128:512], [ra], [Bko.r()])
                DMA("sp", nkv[l, tt * 128:(tt + 1) * 128, :], ko[:], [Bko.r()], [])
        stage('kv_b')
        for (c0, dcol, norm) in ((0, 0, True), (256, 2560, False)):
            for tile in range(2):
                pa, ra = ps_mm()
                for kc in range(8):
                    MM(pa[:], v[:, kc, c0:c0 + 128], h[:, kc, tile * 512:(tile + 1) * 512], kc == 0, kc == 7,
                       rw + [Bh.r(kc, tile)], [ra])
                qk_epilogue(pa, ra, kvst[:, dcol + tile * 512:dcol + (tile + 1) * 512], [Bkvst.r("k", dcol, tile)],
                            l, qkgc[:, l * 2 + 1:l * 2 + 2] if norm else None, s == 1, tile)
        stage('kv_c')
        allr = ([Bkvst.r("ob"), Bkvst.r("oc")] + Bkvst.rs(["vb", "vc"], range(8)) + Bkvst.rs(["k"], [0, 2560], [0, 1]))
        Bd = [S.buf(f"kvd{l}{s}{m}") for m in range(2)]
        for m in range(2):
            DMA("sp", kvd[l][s][m], kvst[:, m * 2560:(m + 1) * 2560], allr, [Bd[m].r()])
        stage('kv_d')
        return Bd

    lset = [[region(f"l{i}_{j}", ZO + (i * 4 + j) * 2048, [128, 512], F32) for j in range(4)] for i in range(2)]
    gyb3 = [(at([128, 1024], BF16, slot_off[i] + 4096), S.buf(f"gyb3_{i}", slot_off[i] + 4096, slot_off[i] + 6144)) for i in range(NSLOT)]
    lset3 = [[(at([128, 512], F32, slot_off[i] + j * 2048), S.buf(f"l3_{i}_{j}", slot_off[i] + j * 2048, slot_off[i] + (j + 1) * 2048))
              for j in range(4)] for i in range(NSLOT)]
    xc = [region(f"xc{i}", ZO + 16384 + i * 4096, [128, 1024], F32) for i in range(2)]
    xcb = [region(f"xcb{i}", ZO + 24576 + i * 2048, [128, 1024], BF16) for i in range(2)]
    gyb, Bgyb = region("gyb", ZO + 28672, [128, 1024], BF16)
    HF = [region(f"HF{i}", ZO + 30720, [128, 1024], F32) for i in range(1)]
    obuf, Bo = region("obuf", YO, [128, 8, 1024], BF16)

    def lru(l, s):
        nseq, L = (4, 256) if s == 0 else (1, 1024)
        xa = xaP if s == 0 else xaS
        wi = next_slot()
        wv = sv(wi, [128, 4, 8, 128])
        for d in range(2):
            DMA("pool", wv[:, 2 * d, :, :], lru_wa[l, d].rearrange("n c d -> c n d"), [], [Bslot[wi].r(2 * d)])
            DMA("pool", wv[:, 2 * d + 1, :, :], lru_wx[l, d].rearrange("n c d -> c n d"), [], [Bslot[wi].r(2 * d + 1)])
        yi = next_slot()
        yarea = [sv(yi, [128, 2, 8, 128])[:, k_] for k_ in range(2)]
        li = next_slot()
        sets3 = [lset[0], lset[1], lset3[li]]
        gy2 = [(gyb, Bgyb), gyb3[yi]]

        def prologue(n):
            xcn, Bxc = xc[n % 2]
            xbn, Bxb = xcb[n % 2]
            gy, Bgy = gy2[n % 2]
            xaR = Bxa.rs([n], [0, 1, "h"])
            cw = lambda k: PT[:, PRM["lcw"] + (l * 4 + k) * 8 + n:PRM["lcw"] + (l * 4 + k) * 8 + n + 1]
            cb = PT[:, PRM["lcb"] + l * 8 + n:PRM["lcb"] + l * 8 + n + 1]
            xcv = xcn[:].rearrange("p (a b) -> p a b", a=nseq)
            TS("dve", xcv, xa[:, n, :, 0:L], cw(0), ALU.mult, xaR + RPT, [Bxc.r()], s2=cb, op1=ALU.add)
            for k in range(1, 4):
                STT("dve", xcv, xa[:, n, :, k:k + L], cw(k), xcv, ALU.mult, ALU.add, xaR + RPT + [Bxc.r()], [Bxc.r()])
            CP("act", xbn[:], xcn[:], [Bxc.r()], [Bxb.r()])
            yv = yarea[n % 2]
            yr = [Bslot[yi].r(n % 2)]
            DMA("pool", yv, w_in[l, :, C_YA + n * 128:C_YA + (n + 1) * 128].rearrange("(k p) n -> p k n", p=128), [], yr)
            for tile in range(2):
                pa, ra = ps_mm()
                for kc in range(8):
                    MM(pa[:], yv[:, kc, :], h[:, kc, tile * 512:(tile + 1) * 512], kc == 0, kc == 7,
                       yr + [Bh.r(kc, tile)], [ra])
                ACT(gy[:, tile * 512:(tile + 1) * 512], pa[:], AF.Gelu_apprx_tanh, [ra], [Bgy.r(tile)])

        def front(n, d, hi, st_):
            xbn, Bxb = xcb[n % 2]
            ci = (l * 2 + d) * 8 + n
            half = hi if d == 0 else 1 - hi
            tk = slice(half * 512, (half + 1) * 512)
            (Rt, BR), (It, BI), (At, BA), (St, BS) = st_
            pr, rr = ps_mm()
            MM(pr[:], wv[:, 2 * d, n, :], xbn[:, tk], True, True, [Bslot[wi].r(2 * d), Bxb.r()], [rr])
            pi, ri = ps_mm()
            MM(pi[:], wv[:, 2 * d + 1, n, :], xbn[:, tk], True, True, [Bslot[wi].r(2 * d + 1), Bxb.r()], [ri])
            ACT(Rt[:], pr[:], AF.Tanh, [rr, Bhb.r()], [BR.r()], bias=hbias[:, 0, ci:ci + 1], scale=0.5)
            ACT(It[:], pi[:], AF.Tanh, [ri, Bhb.r()], [BI.r()], bias=hbias[:, 1, ci:ci + 1], scale=0.5)
            ACT(At[:], Rt[:], AF.Exp, [BR.r(), Bcn.r()], [BA.r()], bias=cn[:, 0, ci:ci + 1], scale=cn[:, 0, ci:ci + 1])
            if s == 0:
                TT("dve", St[:], At[:], At[:], ALU.mult, [BA.r()], [BS.r()])
            else:
                ACT(St[:], Rt[:], AF.Exp, [BR.r(), Bcn.r()], [BS.r()], bias=cn[:, 1, ci:ci + 1], scale=cn[:, 1, ci:ci + 1])

        def mid(st_):
            (Rt, BR), (It, BI), (At, BA), (St, BS) = st_
            ACT(St[:], St[:], AF.Sqrt, [BS.r()], [BS.r()], bias=0.25, scale=-0.25)

        def back(n, d, hi, st_):
            xcn, Bxc = xc[n % 2]
            hf, Bhf = HF[0]
            gy, Bgy = gy2[n % 2]
            xaR = Bxa.rs([n], [0, 1, "h"])
            half = hi if d == 0 else 1 - hi
            tk = slice(half * 512, (half + 1) * 512)
            (Rt, BR), (It, BI), (At, BA), (St, BS) = st_
            if True:
                if True:
                    STT("dve", It[:], It[:], 1.0, xcn[:, tk], ALU.add, ALU.mult, [BI.r(), Bxc.r()], [BI.r()])
                    TT("dve", It[:], It[:], St[:], ALU.mult, [BI.r(), BS.r()], [BI.r()])
                    segs = [(q, (q % 2) * 256, 256) for q in range(4) if q // 2 == half] if s == 0 else [(0, 0, 512)]
                    for (q, lo, n_) in segs:
                        if s == 0 or hi == 0:
                            init, initR = 0.0, []
                            pinit, pinitR = 1.0, []
                        else:
                            init, initR = carry[:, d, n:n + 1], [Bcar.r(d, n)]
                            pinit, pinitR = carryP[:, d, n:n + 1], [BcarP.r(d, n)]
                        if d == 0:
                            outap = hf[:, half * 512 + lo:half * 512 + lo + n_]
                            a_ap, u_ap = At[:, lo:lo + n_], It[:, lo:lo + n_]
                            outR = Bhf.r(half)
                            p_out = St[:, lo:lo + n_]
                        else:
                            outap = Rt[:, lo:lo + n_][:, ::-1]
                            a_ap, u_ap = At[:, lo:lo + n_][:, ::-1], It[:, lo:lo + n_][:, ::-1]
                            outR = BR.r()
                            p_out = St[:, lo:lo + n_][:, ::-1]
                        S.add("dve", lambda e, o=outap, a=a_ap, u=u_ap, i0=init: e.tensor_tensor_scan(
                            out=o, data0=a, data1=u, initial=i0, op0=ALU.mult, op1=ALU.add),
                            reads=[BA.r(), BI.r()] + initR, writes=[outR])
                        if s == 0:
                            if d == 0:
                                CP("dve", lst[:, q, 0, n:n + 1], hf[:, q * 256 + 255:q * 256 + 256], [outR], [Blst.r(q, 0, n)])
                            else:
                                CP("dve", lst[:, q, 1, n:n + 1], Rt[:, lo:lo + 1], [outR], [Blst.r(q, 1, n)])
                        else:
                            S.add("dve", lambda e, o=p_out, a=a_ap, i0=pinit: e.tensor_tensor_scan(
                                out=o, data0=a, data1=a, initial=i0, op0=ALU.mult, op1=ALU.min),
                                reads=[BA.r()] + pinitR, writes=[BS.r()])
                            hsrc = hf[:, half * 512 + 511:half * 512 + 512] if d == 0 else Rt[:, 0:1]
                            psrc = St[:, 511:512] if d == 0 else St[:, 0:1]
                            if hi == 0:
                                CP("dve", carry[:, d, n:n + 1], hsrc, [outR], [Bcar.r(d, n)])
                                CP("dve", carryP[:, d, n:n + 1], psrc, [BS.r()], [BcarP.r(d, n)])
                            else:
                                CP("dve", ccc[:, d, n, 1:2], hsrc, [outR], [Bccc.r(d, n, 1)])
                                CP("dve", ccc[:, d, n, 0:1], psrc, [BS.r()], [Bccc.r(d, n, 0)])
                            TT("dve", xaG[:, n, d * 1024 + half * 512:d * 1024 + (half + 1) * 512], St[:], gy[:, tk], ALU.mult,
                               [BS.r(), Bgy.r(half)] + xaR, [BxaGl[n].r(d, half)])
                    if d == 1:
                        TT("dve", Rt[:], Rt[:], hf[:, tk], ALU.add, [BR.r(), Bhf.r(half)], [BR.r()])
                        TT("dve", obuf[:, n, tk], Rt[:], gy[:, tk], ALU.mult, [BR.r(), Bgy.r(half)], [Bo.r(n, half)])


        items = [(n, d, hi) for n in range(8) for d in range(2) for hi in range(2)]
        prologue(0)
        k = 0
        while k < len(items):
            pair = items[k:k + 2]
            sts = [sets3[(k + i_) % 3] for i_ in range(len(pair))]
            for (n_, d_, hi_), st_ in zip(pair, sts):
                front(n_, d_, hi_, st_)
            for st_ in sts:
                mid(st_)
            if pair[0][1] == 0 and pair[0][0] + 1 < 8:
                prologue(pair[0][0] + 1)
            for (n_, d_, hi_), st_ in zip(pair, sts):
                back(n_, d_, hi_, st_)
            k += 2

    def lru_fix(l):
        for n in range(8):
            for d in range(2):
                for half in range(2):
                    tk = slice(half * 512, (half + 1) * 512)
                    STT("dve", obuf[:, n, tk], xaG[:, n, d * 1024 + half * 512:d * 1024 + (half + 1) * 512], hst[:, d, n:n + 1], obuf[:, n, tk],
                        ALU.mult, ALU.add, [BxaGl[n].r(d, half), Bhst.r(), Bo.r(n, half)], [Bo.r(n, half)])

    merged, Bmg = region("merged", XO, [128, 8, 1024], F32)
    mergedb, Bmgb = region("mergedb", ZO, [128, 8, 1024], BF16)
    pth = [region(f"pth{i}", ZO + 16384 + i * 2048, [128, 512], F32) for i in range(2)]
    ptm = [region(f"ptm{i}", ZO + 20480 + i * 2048, [128, 512], F32) for i in range(2)]

    pthA = [region(f"pthA{i}", ZO + 8192 + i * 2048, [128, 512], F32) for i in range(2)]

    def branch_proj(l, which, w_o, kch, gcol, ochunks):
        cnt = 0
        pth_ = pthA if which == 0 else pth
        for ob in range(2):
            if kch == 8:
                oi, ov, orr = load_w(w_o[l, :, ob * 512:(ob + 1) * 512], 8, 512)
            else:
                oi = next_slot()
                ov = sv(oi, [128, 4, 512])
                for half in range(2):
                    DMA("pool", ov[half * 64:(half + 1) * 64, :, :],
                        w_o[l, half * 256:(half + 1) * 256, ob * 512:(ob + 1) * 512].rearrange("(j p) n -> p j n", p=64),
                        [], [Bslot[oi].r(half)])
                orr = [Bslot[oi].r(0), Bslot[oi].r(1)]
            gi, gv, gr = load_w(w_in[l, :, gcol + ob * 512:gcol + (ob + 1) * 512], 8, 512)
            for o4 in range(4):
                oc = ob * 4 + o4
                for tile in range(2):
                    tk = slice(tile * 512, (tile + 1) * 512)
                    po, ro = ps_mm()
                    for kc in range(kch):
                        MM(po[:], ov[:, kc, o4 * 128:(o4 + 1) * 128], obuf[:, ochunks[kc], tk], kc == 0, kc == kch - 1,
                           orr + [Bo.r(ochunks[kc], tile)], [ro])
                    pg, rg = ps_mm()
                    for kc in range(8):
                        MM(pg[:], gv[:, kc, o4 * 128:(o4 + 1) * 128], h[:, kc, tk], kc == 0, kc == 7, gr + [Bh.r(kc, tile)], [rg])
                    th, Bth = pth_[cnt % 2]
                    tm, Btm = ptm[cnt % 2]
                    cnt += 1
                    ACT(th[:], pg[:], AF.Tanh, [rg], [Bth.r()], scale=0.5)
                    if which == 0:
                        STT("dve", merged[:, oc, tk], th[:], 1.0, po[:], ALU.add, ALU.mult, [Bth.r(), ro], [Bmg.r(oc, tile)])
                    else:
                        STT("dve", tm[:], th[:], 1.0, po[:], ALU.add, ALU.mult, [Bth.r(), ro], [Btm.r()])
                        if which == 1:
                            TT("dve", merged[:, oc, tk], merged[:, oc, tk], tm[:], ALU.add, [Bmg.r(oc, tile), Btm.r()], [Bmg.r(oc, tile)])
                        else:
                            TT("dve", mergedb[:, oc, tk], merged[:, oc, tk], tm[:], ALU.add, [Bmg.r(oc, tile), Btm.r()], [Bmgb.r(oc, tile)])

    def out_proj_residual(l, s, w_ap, kch, src, srcR, gcol):
        ncb = 4 if kch == 8 else 1
        nob = 8 // ncb
        for ob in range(nob):
            wi_, wv_, wr_ = load_w(w_ap[l, :, ob * ncb * 128:(ob + 1) * ncb * 128], kch, ncb * 128)
            for o4 in range(ncb):
                oc = ob * ncb + o4
                for tile in range(2):
                    tk = slice(tile * 512, (tile + 1) * 512)
                    xt = slice(s * T + tile * 512, s * T + (tile + 1) * 512)
                    po, ro = ps_mm()
                    for kc in range(kch):
                        MM(po[:], wv_[:, kc, o4 * 128:(o4 + 1) * 128], src[:, kc, tk], kc == 0, kc == kch - 1,
                           wr_ + [srcR(kc, tile)], [ro])
                    STT("dve", xres[:, oc, xt], po[:], gcol(oc), xres[:, oc, xt], ALU.mult, ALU.add,
                        [ro, Bx.r(s, oc, tile)] + RC, [Bx.r(s, oc, tile)])

    KT, BKT = region("KT", ZO, [128, 4352], BF16)
    VV, BVV = region("VV", ZO + 8704, [128, 34, 192], BF16)
    QT, BQT = region("QT", ZO + 21760, [128, 4, 1024], BF16)
    PTl = [region(f"PTl{i}", ZO + 29952 + i * 2048, [128, 1024], BF16) for i in range(2)]
    rdb = [region(f"rdb{i}", YO + 8192 + i * 2048, [128, 512], F32) for i in range(2)]
    cst, Bcst = region("cst", YO + 12288, [128, 2, 128], F32)
    candK, BcK = region("candK", ZO + 3072, [128, 2, 4, 128], BF16)
    candV, BcV = region("candV", ZO + 8704 + 4608, [128, 2, 4, 192], BF16)
    _pt = [0]

    def next_pt():
        i = _pt[0] % 2
        _pt[0] += 1
        return PTl[i]

    def q_proj(l, s, qcol, normcol, rope):
        qi = next_slot()
        qv = sv(qi, [128, 8, 512])
        q5 = sv(qi, [128, 8, 4, 2, 64])
        qr = []
        for half in range(2):
            for j in range(4):
                DMA("pool", qv[:, :, j * 128 + half * 64:j * 128 + half * 64 + 64],
                    w_in[l, :, qcol + half * 256 + j * 64:qcol + half * 256 + j * 64 + 64].rearrange("(k p) d -> p k d", p=128),
                    [], [Bslot[qi].r(half, j)])
                qr.append(Bslot[qi].r(half, j))
        for j in range(4):
            for tile in range(2):
                pa, ra = ps_mm()
                for kc in range(8):
                    MM(pa[:], qv[:, kc, j * 128:(j + 1) * 128], h[:, kc, tile * 512:(tile + 1) * 512], kc == 0, kc == 7, qr + [Bh.r(kc, tile)], [ra])
                qk_epilogue(pa, ra, QT[:, j, tile * 512:(tile + 1) * 512], [BQT.r(j, tile)], l, normcol, rope, tile)

    _fin = [0]

    def finalize(acc, racc, half, ncols, dst, dstR, escol, on_dve=False):
        rd, Brd = rdb[_fin[0] % 2]
        _fin[0] += 1
        dp = slice(64, 128) if half == 0 else slice(0, 64)
        npp = slice(0, 64) if half == 0 else slice(64, 128)
        if on_dve:
            S.add("dve", lambda e: e.reciprocal(out=rd[dp, 0:ncols], in_=acc[dp, 0:ncols]), reads=[racc], writes=[Brd.r()])
            TT("dve", dst, acc[npp, 0:ncols], rd[dp, 0:ncols], ALU.mult, [racc, Brd.r()], dstR)
            return
        if escol is not None:
            ACT(rd[dp, 0:ncols], acc[dp, 0:ncols], AF.Ln, [racc, Bes.r()], [Brd.r()], bias=escol[dp], scale=1.0)
        else:
            ACT(rd[dp, 0:ncols], acc[dp, 0:ncols], AF.Ln, [racc], [Brd.r()])
        ACT(rd[dp, 0:ncols], rd[dp, 0:ncols], AF.Exp, [Brd.r()], [Brd.r()], scale=-1.0)
        TT("dve", dst, acc[npp, 0:ncols], rd[dp, 0:ncols], ALU.mult, [racc, Brd.r()], dstR)

    def load_kv_local(Bd, src, kcol, vcol, kdst, vdst_blk0, nblk=8):
        DMA("sp", KT[:, kdst:kdst + 1024], src[:, kcol:kcol + 1024], [Bd.r()], [BKT.r("loc")])
        DMA("sp", VV[:, vdst_blk0:vdst_blk0 + nblk, :], src[:, vcol:vcol + nblk * 192].rearrange("p (b c) -> p b c", c=192),
            [Bd.r()], [BVV.r("loc")])

    def run_pipeline(steps):
        n = len(steps)
        if n == 0:
            return
        steps[0]["score"]()
        for i in range(n):
            steps[i]["exp"]()
            if i + 1 < n:
                steps[i + 1]["score"]()
            steps[i]["pv"]()
            if steps[i].get("fin"):
                steps[i]["fin"]()

    _sc = [0]
    _ac = [0]

    def next_score():
        i = _sc[0] % 2
        _sc[0] += 1
        return pb2[i], [Bbank[2 * i].r(), Bbank[2 * i + 1].r()], [bank[2 * i], bank[2 * i + 1]]

    def next_accpair():
        i = 2 + (_ac[0] % 2)
        _ac[0] += 1
        return [bank[2 * i], bank[2 * i + 1]], [Bbank[2 * i].r(), Bbank[2 * i + 1].r()]

    def attn_P(l, mixer, ooff):
        steps = []
        for tile_ in range(2):
          for j in range(4):
            grp = {}
            for seq in (2 * tile_, 2 * tile_ + 1):
                def mk(seq=seq, j=j, grp=grp):
                    st = {}
                    box = {}

                    def score():
                        box["sc"], box["rs"], box["bk"] = next_score()
                        for half in range(2):
                            hp = slice(half * 64, (half + 1) * 64)
                            for kb in range(2):
                                blk = seq * 2 + kb
                                MM(box["bk"][half][:, kb * 256:(kb + 1) * 256], KT[hp, blk * 128:(blk + 1) * 128],
                                   QT[hp, j, seq * 256:(seq + 1) * 256], True, True, [BKT.r("loc"), BQT.r(j, seq // 2)], [box["rs"][half]])

                    def exp():
                        box["pt"], box["Bpt"] = next_pt()
                        ACT(box["pt"][:, 0:1024], box["sc"][:, 0:1024], AF.Exp, box["rs"], [box["Bpt"].r()], scale=0.125)

                    def pv():
                        if seq % 2 == 0:
                            grp["accs"], grp["raccs"] = next_accpair()
                        accs, raccs = grp["accs"], grp["raccs"]
                        box["accs"], box["raccs"] = accs, raccs
                        for half in range(2):
                            for kb in range(2):
                                blk = seq * 2 + kb
                                MM(accs[half][:, (seq % 2) * 256:(seq % 2 + 1) * 256], VV[:, blk, half * 64:half * 64 + 128],
                                   box["pt"][:, half * 512 + kb * 256:half * 512 + (kb + 1) * 256], kb == 0, kb == 1,
                                   [BVV.r("loc"), box["Bpt"].r()], [raccs[half]])

                    def fin():
                        for half in range(2):
                            hp = slice(half * 64, (half + 1) * 64)
                            hd = j + 4 * half
                            finalize(box["accs"][half], box["raccs"][half], half, 512, obuf[hp, ooff + j, (seq // 2) * 512:(seq // 2 + 1) * 512],
                                     [Bo.r(ooff + j, seq // 2, "a", 0, half)],
                                     es[:, l * 8 + hd:l * 8 + hd + 1] if mixer == "c" else None)
                    st["score"], st["exp"], st["pv"], st["fin"] = score, exp, pv, (fin if seq % 2 == 1 else None)
                    return st
                steps.append(mk())
        run_pipeline(steps)

    def attn_SB(l):
        steps = []
        for j in range(4):
            for qt in range(2):
                grp = {}
                for kb in range(34):
                    def mk(j=j, qt=qt, kb=kb, grp=grp):
                        box = {}

                        def score():
                            box["sc"], box["rs"], box["bk"] = next_score()
                            for half in range(2):
                                hp = slice(half * 64, (half + 1) * 64)
                                MM(box["bk"][half], KT[hp, kb * 128:(kb + 1) * 128], QT[hp, j, qt * 512:(qt + 1) * 512], True, True,
                                   [BKT.r("loc"), BKT.r("ctx"), BQT.r(j, qt)], [box["rs"][half]])

                        def exp():
                            box["pt"], box["Bpt"] = next_pt()
                            ACT(box["pt"][:, 0:1024], box["sc"][:, 0:1024], AF.Exp, box["rs"], [box["Bpt"].r()], scale=0.125)

                        def pv():
                            if kb == 0:
                                grp["accs"], grp["raccs"] = next_accpair()
                            for half in range(2):
                                MM(grp["accs"][half], VV[:, kb, half * 64:half * 64 + 128], box["pt"][:, half * 512:(half + 1) * 512],
                                   kb == 0, kb == 33, [BVV.r("loc"), BVV.r("ctx"), box["Bpt"].r()], [grp["raccs"][half]])

                        def fin():
                            for half in range(2):
                                hp = slice(half * 64, (half + 1) * 64)
                                finalize(grp["accs"][half], grp["raccs"][half], half, 512, obuf[hp, j, qt * 512:(qt + 1) * 512],
                                         [Bo.r(j, qt, "a", 0, half)], None, on_dve=True)
                        return {"score": score, "exp": exp, "pv": pv, "fin": fin if kb == 33 else None}
                    steps.append(mk())
        run_pipeline(steps)

    def attn_SC(l, ooff):
        steps = []
        for j in range(4):
            grps = [{}, {}]
            for n in range(8):
                def mk(j=j, n=n, grp=grps[n // 4]):
                    box = {"grp": grp}
                    qs = slice(n * 128, (n + 1) * 128)
                    kcols = [0, 128, 256 + n * 128, 384 + n * 128, 512 + n * 128]
                    vblks = [0, 1, 2 + n, 3 + n, 4 + n]
                    mp = 2 if n == 0 else 0
                    mn = 3 if n == 7 else 1

                    def score():
                        box["sc"], box["rs"], box["bk"] = next_score()
                        for half in range(2):
                            hp = slice(half * 64, (half + 1) * 64)
                            for i5 in range(4):
                                MM(box["bk"][half][:, i5 * 128:(i5 + 1) * 128], KT[hp, kcols[i5]:kcols[i5] + 128], QT[hp, j, qs], True, True,
                                   [BKT.r("loc"), BKT.r("ctx"), BKT.r("halo"), BQT.r(j, n // 4)], [box["rs"][half]])

                    def exp():
                        box["pt"], box["Bpt"] = next_pt()
                        ACT(box["pt"][:, 0:1024], box["sc"][:, 0:1024], AF.Exp, box["rs"], [box["Bpt"].r()], scale=0.125)
                        for half in range(2):
                            TT("dve", box["pt"][:, half * 512 + 256:half * 512 + 384], box["pt"][:, half * 512 + 256:half * 512 + 384], maskb[:, mp, :], ALU.mult,
                               [box["Bpt"].r(), Bmask.r()], [box["Bpt"].r()])

                    def pv():
                        if n % 4 == 0:
                            grp["accs"], grp["raccs"] = next_accpair()
                        accs, raccs = grp["accs"], grp["raccs"]
                        box["accs"], box["raccs"] = accs, raccs
                        for half in range(2):
                            for i5 in range(4):
                                MM(accs[half][:, (n % 4) * 128:(n % 4 + 1) * 128], VV[:, vblks[i5], half * 64:half * 64 + 128],
                                   box["pt"][:, half * 512 + i5 * 128:half * 512 + (i5 + 1) * 128], i5 == 0, False,
                                   [BVV.r("loc"), BVV.r("ctx"), BVV.r("halo"), box["Bpt"].r()], [raccs[half]])
                    return {"score": score, "exp": exp, "pv": pv, "box": box, "kcols": kcols, "vblks": vblks, "mn": mn, "qs": qs, "j": j, "n": n}
                st = mk()
                def mk2(st=st, j=j, n=n):
                    box2 = {}
                    box = st["box"]
                    qs, kcols, vblks, mn = st["qs"], st["kcols"], st["vblks"], st["mn"]

                    def score():
                        box2["sc"], box2["rs"], box2["bk"] = next_score()
                        for half in range(2):
                            hp = slice(half * 64, (half + 1) * 64)
                            MM(box2["bk"][half][:, 0:128], KT[hp, kcols[4]:kcols[4] + 128], QT[hp, j, qs], True, True,
                               [BKT.r("loc"), BKT.r("ctx"), BKT.r("halo"), BQT.r(j, n // 4)], [box2["rs"][half]])

                    def exp():
                        box2["pt"], box2["Bpt"] = next_pt()
                        ACT(box2["pt"][:, 0:256].rearrange("p (a b) -> p a b", a=2),
                            box2["sc"][:, 0:1024].rearrange("p (a b) -> p a b", a=2)[:, :, 0:128], AF.Exp, box2["rs"], [box2["Bpt"].r()], scale=0.125)
                        for half in range(2):
                            TT("dve", box2["pt"][:, half * 128:(half + 1) * 128], box2["pt"][:, half * 128:(half + 1) * 128], maskb[:, mn, :], ALU.mult,
                               [box2["Bpt"].r(), Bmask.r()], [box2["Bpt"].r()])

                    def pv():
                        for half in range(2):
                            MM(box["accs"][half][:, (n % 4) * 128:(n % 4 + 1) * 128], VV[:, vblks[4], half * 64:half * 64 + 128],
                               box2["pt"][:, half * 128:(half + 1) * 128], False, True,
                               [BVV.r("loc"), BVV.r("ctx"), BVV.r("halo"), box2["Bpt"].r()], [box["raccs"][half]])

                    def fin():
                        for half in range(2):
                            hp = slice(half * 64, (half + 1) * 64)
                            hd = j + 4 * half
                            finalize(box["accs"][half], box["raccs"][half], half, 512, obuf[hp, ooff + j, (n // 4) * 512:(n // 4 + 1) * 512],
                                     [Bo.r(ooff + j, n // 4, "a", 0, half)], es[:, l * 8 + hd:l * 8 + hd + 1])
                    return {"score": score, "exp": exp, "pv": pv, "fin": fin if n % 4 == 3 else None}
                steps.append(st)
                steps.append(mk2())
        run_pipeline(steps)

    def load_ctx(l, kc_ap, vc_ap):
        DMA("sp", cst[:], kc_ap[l].rearrange("(b p) c -> p b c", p=128), [], [Bcst.r()])
        pa, ra = ps_mm()
        for b in range(2):
            TR(pa[:, b * 128:(b + 1) * 128], cst[:, b, :], [Bcst.r()], [ra])
        CP("act", KT[:, 0:256], pa[:, 0:256], [ra], [BKT.r("ctx")])
        DMA("sp", cst[:], vc_ap[l].rearrange("(b p) c -> p b c", p=128), [ra], [Bcst.r()])
        MSET("dve", VV[:, 0:2, 64:128], 1.0, [BVV.r("ctx")])
        CP("dve", VV[:, 0:2, :].rearrange("p b (a c) -> p b a c", c=64)[:, :, 0:3:2, :], cst[:].rearrange("p b (a c) -> p b a c", c=64),
           [Bcst.r()], [BVV.r("ctx")])

    def obufR_full(chunks):
        def f(kc, tile):
            return None
        return f

    actb, Bact = region("actb", YO, [128, 22, 1024], BF16)
    gpP = [at([128, 4, 258], F32, XO + 14336 + i * 4160) for i in range(2)]
    gpS = [at([128, 1, 1026], F32, XO + 14336 + i * 4160) for i in range(2)]
    Bgp = [S.buf(f"gp{i}", XO + 14336 + i * 4160, XO + 14336 + (i + 1) * 4160) for i in range(2)]
    cvbuf = [region(f"cv{i}", XO + 22656 + i * 4096, [128, 1024], F32) for i in range(2)]
    glb = [region(f"gl{i}", XO + 30848 + i * 1216, [128, 512], BF16) for i in range(2)]

    def ffn(l, s):
        nseq, L = (4, 256) if s == 0 else (1, 1024)
        fw = lambda k, j: PT[:, PRM["fcw"] + (l * 3 + k) * 22 + j:PRM["fcw"] + (l * 3 + k) * 22 + j + 1]
        fb = lambda j: PT[:, PRM["fcb"] + l * 22 + j:PRM["fcb"] + l * 22 + j + 1]
        for j in range(22):
            i = next_slot()
            v = sv(i, [128, 8, 256])
            DMA("pool", v[:, :, 0:128], w_up[l, :, j * 128:(j + 1) * 128].rearrange("(k p) n -> p k n", p=128), [], [Bslot[i].r(0)])
            DMA("pool", v[:, :, 128:256], w_up[l, :, 2816 + j * 128:2816 + (j + 1) * 128].rearrange("(k p) n -> p k n", p=128), [], [Bslot[i].r(1)])
            gp = (gpP if s == 0 else gpS)[j % 2]
            Bg = Bgp[j % 2]
            cv, Bcv = cvbuf[j % 2]
            MSET("dve", gp[:, :, 0:1], 0.0, [Bg.r("h")])
            MSET("dve", gp[:, :, L + 1:L + 2], 0.0, [Bg.r("h")])
            for tile in range(2):
                pa, ra = ps_mm()
                for kc in range(8):
                    MM(pa[:], v[:, kc, 0:128], h[:, kc, tile * 512:(tile + 1) * 512], kc == 0, kc == 7, [Bslot[i].r(0), Bh.r(kc, tile)], [ra])
                if s == 0:
                    CP("act", gp[:, tile * 2:tile * 2 + 2, 1:257], pa[:].rearrange("p (a b) -> p a b", a=2), [ra], [Bg.r(tile)])
                else:
                    CP("act", gp[:, 0, 1 + tile * 512:1 + (tile + 1) * 512], pa[:], [ra], [Bg.r(tile)])
            gR = [Bg.r(0), Bg.r(1), Bg.r("h")]
            cvv = cv[:].rearrange("p (a b) -> p a b", a=nseq)
            ACT(cvv, gp[:, :, 0:L], AF.Identity, gR + RPT, [Bcv.r()], bias=fb(j), scale=fw(0, j))
            STT("dve", cvv, gp[:, :, 1:L + 1], fw(1, j), cvv, ALU.mult, ALU.add, gR + RPT + [Bcv.r()], [Bcv.r()])
            STT("dve", cvv, gp[:, :, 2:L + 2], fw(2, j), cvv, ALU.mult, ALU.add, gR + RPT + [Bcv.r()], [Bcv.r()])
            if s == 1:
                CP("dve", eg[:, j, 0:2], gp[:, 0, 1:3], gR, [Beg.r(j)])
                CP("dve", eg[:, j, 2:4], gp[:, 0, 1023:1025], gR, [Beg.r(j)])
            for tile in range(2):
                gl, Bgl = glb[tile]
                ACT(gl[:], cv[:, tile * 512:(tile + 1) * 512], AF.Gelu_apprx_tanh, [Bcv.r()], [Bgl.r()])
                pv, rv = ps_mm()
                for kc in range(8):
                    MM(pv[:], v[:, kc, 128:256], h[:, kc, tile * 512:(tile + 1) * 512], kc == 0, kc == 7, [Bslot[i].r(1), Bh.r(kc, tile)], [rv])
                TT("dve", actb[:, j, tile * 512:(tile + 1) * 512], gl[:], pv[:], ALU.mult, [Bgl.r(), rv], [Bact.r(j, tile)])
                if s == 1:
                    col = 0 if tile == 0 else 511
                    CP("dve", ev[:, j, tile:tile + 1], pv[:, col:col + 1], [rv], [Bev.r(j)])
        if s == 1:
            egR = Beg.rs(range(22))
            CP("dve", ge[:, :, 0], eg[:, :, 0], egR, [Bge.r()])
            CP("dve", ge[:, :, 1], eg[:, :, 3], egR, [Bge.r()])
            Bdi, Bdo = S.buf(f"ccDi{l}"), S.buf(f"ccDo{l}")
            DMA("sp", ccD_in[l], ge[:].rearrange("p a b -> p (a b)"), [Bge.r()], [Bdi.r()])
            AG(ccD_in[l], ccD_out[l], [Bdi.r()], [Bdo.r()])
            DMA("sp", gg[:].rearrange("p r a b -> p r (a b)"), ccD_out[l].rearrange("(r p) n -> p r n", p=128), [Bdo.r()], [Bgg.r()])
            for (hh, so, e_) in ((0, 0, 1), (1, 4, 0)):
                TS("dve", ehal[:, hh, :], gg[:, 0, :, e_], sel[:, so:so + 1], ALU.mult, [Bgg.r(), Bsel.r()], [Behal.r()])
                for r_ in range(1, 4):
                    STT("dve", ehal[:, hh, :], gg[:, r_, :, e_], sel[:, so + r_:so + r_ + 1], ehal[:, hh, :], ALU.mult, ALU.add,
                        [Bgg.r(), Bsel.r(), Behal.r()], [Behal.r()])
            W0 = PT[:, PRM["fcw"] + (l * 3 + 0) * 22:PRM["fcw"] + (l * 3 + 0) * 22 + 22]
            W1 = PT[:, PRM["fcw"] + (l * 3 + 1) * 22:PRM["fcw"] + (l * 3 + 1) * 22 + 22]
            W2 = PT[:, PRM["fcw"] + (l * 3 + 2) * 22:PRM["fcw"] + (l * 3 + 2) * 22 + 22]
            FB = PT[:, PRM["fcb"] + l * 22:PRM["fcb"] + l * 22 + 22]
            evR = Bev.rs(range(22))
            for (e_, a0, a1, a2, tokc) in ((0, ehal[:, 0, :], eg[:, :, 0], eg[:, :, 1], 0), (1, eg[:, :, 2], eg[:, :, 3], ehal[:, 1, :], 1023)):
                t0 = etmp[:, e_ * 2, :]
                t1 = etmp[:, e_ * 2 + 1, :]
                RR = [Behal.r(), Betmp.r(e_)] + egR + RPT
                TT("dve", t0, a0, W0, ALU.mult, RR, [Betmp.r(e_)])
                TT("dve", t1, a1, W1, ALU.mult, RR, [Betmp.r(e_)])
                TT("dve", t0, t0, t1, ALU.add, RR, [Betmp.r(e_)])
                TT("dve", t1, a2, W2, ALU.mult, RR, [Betmp.r(e_)])
                TT("dve", t0, t0, t1, ALU.add, RR, [Betmp.r(e_)])
                TT("dve", t0, t0, FB, ALU.add, RR, [Betmp.r(e_)])
                ACT(t1, t0, AF.Gelu_apprx_tanh, [Betmp.r(e_)], [Betmp.r(e_)])
                TT("dve", actb[:, :, tokc], t1, ev[:, :, e_], ALU.mult, [Betmp.r(e_)] + evR + Bact.rs(range(22), [e_]), Bact.rs(range(22), [e_]))

    def sample_exchange_kv(l, Bd):
        Bao = [S.buf(f"ccAo{l}{m}") for m in range(2)]
        for m in range(2):
            AG(kvd[l][1][m], ccA_out[l][m], [Bd[m].r()], [Bao[m].r()])
        return Bao

    def sample_exchange_xa_issue(l):
        CP("dve", xe[:, :, 0:1], xaS[:, :, 0, 2:3], Bxa.rs(range(8), [0]), [Bxe.r()])
        CP("dve", xe[:, :, 1:3], xaS[:, :, 0, 1024:1026], Bxa.rs(range(8), [1]), [Bxe.r()])
        Bbi, Bbo = S.buf(f"ccBi{l}"), S.buf(f"ccBo{l}")
        DMA("sp", ccB_in[l], xe[:].rearrange("p a b -> p (a b)"), [Bxe.r()], [Bbi.r()])
        AG(ccB_in[l], ccB_out[l], [Bbi.r()], [Bbo.r()])
        return Bbo

    def sample_exchange_xa_finish(l, Bbo):
        DMA("sp", xg[:].rearrange("p r a b -> p r (a b)"), ccB_out[l].rearrange("(r p) n -> p r n", p=128), [Bbo.r()], [Bxg.r()])
        hR_ = Bxa.rs(range(8), ["h"])
        for (dst, so, src) in ((xaS[:, :, 0, 0:2], 0, lambda r_: xg[:, r_, :, 1:3]), (xaS[:, :, 0, 1026:1027], 4, lambda r_: xg[:, r_, :, 0:1])):
            TS("dve", dst, src(0), sel[:, so:so + 1], ALU.mult, [Bxg.r(), Bsel.r()], hR_)
            for r_ in range(1, 4):
                STT("dve", dst, src(r_), sel[:, so + r_:so + r_ + 1], dst, ALU.mult, ALU.add, [Bxg.r(), Bsel.r()] + hR_, hR_)

    def lru_exchange(l):
        allc = Bccc.rs(range(2), range(8), range(2))
        Bci, Bco = S.buf(f"ccCi{l}"), S.buf(f"ccCo{l}")
        DMA("sp", ccC_in[l], ccc[:].rearrange("p d n a -> p (d n a)"), allc, [Bci.r()])
        AG(ccC_in[l], ccC_out[l], [Bci.r()], [Bco.r()])
        return Bco

    def lru_exchange_finish(l, Bco):
        DMA("sp", ccg[:].rearrange("p r d n a -> p r (d n a)"), ccC_out[l].rearrange("(r p) n -> p r n", p=128), [Bco.r()], [Bccg.r()])
        stc = PRM["st"] + l * 16
        CP("dve", HH[:, 0, 0, :], PT[:, stc:stc + 8], RPT, [BHH.r()])
        for j in range(3):
            TT("dve", HH[:, 0, j + 1, :], HH[:, 0, j, :], ccg[:, j, 0, :, 0], ALU.mult, [BHH.r(), Bccg.r()], [BHH.r()])
            TT("dve", HH[:, 0, j + 1, :], HH[:, 0, j + 1, :], ccg[:, j, 0, :, 1], ALU.add, [BHH.r(), Bccg.r()], [BHH.r()])
        CP("dve", HH[:, 1, 3, :], PT[:, stc + 8:stc + 16], RPT, [BHH.r()])
        for j in (3, 2, 1):
            TT("dve", HH[:, 1, j - 1, :], HH[:, 1, j, :], ccg[:, j, 1, :, 0], ALU.mult, [BHH.r(), Bccg.r()], [BHH.r()])
            TT("dve", HH[:, 1, j - 1, :], HH[:, 1, j - 1, :], ccg[:, j, 1, :, 1], ALU.add, [BHH.r(), Bccg.r()], [BHH.r()])
        for d in range(2):
            TS("dve", hst[:, d, :], HH[:, d, 0, :], sel[:, 8:9], ALU.mult, [BHH.r(), Bsel.r()], [Bhst.r()])
            for j in range(1, 4):
                STT("dve", hst[:, d, :], HH[:, d, j, :], sel[:, 8 + j:9 + j], hst[:, d, :], ALU.mult, ALU.add, [BHH.r(), Bsel.r(), Bhst.r()], [Bhst.r()])

    def obR(chunk, tile):
        return Bo.r(chunk, tile)

    for l in range(DEPTH):
        for s in (1, 0):
            cidx = 1 if s == 1 else 0
            stage(f'L{l}s{s} start')
            rmsnorm_h(s, lambda kc: AA[:, l, s, 0, kc:kc + 1], lambda kc: mod[:, l, kc, cidx:cidx + 1])
            stage(f'L{l}s{s} T2 kv')
            Bxa.s = s
            xa = xaP if s == 0 else xaS
            L = 256 if s == 0 else 1024
            if s == 0:
                MSET("dve", xaP[:, :, :, 0:2], 0.0, Bxa.rs(range(8), ["h"]))
                MSET("dve", xaP[:, :, :, 258:259], 0.0, Bxa.rs(range(8), ["h"]))
            for b in range(2):
                xi, xv, xr = load_w(w_in[l, :, C_XA + b * 512:C_XA + (b + 1) * 512], 8, 512)
                for n4 in range(4):
                    n = b * 4 + n4
                    for tile in range(2):
                        pa, ra = ps_mm()
                        for kc in range(8):
                            MM(pa[:], xv[:, kc, n4 * 128:(n4 + 1) * 128], h[:, kc, tile * 512:(tile + 1) * 512], kc == 0, kc == 7,
                               xr + [Bh.r(kc, tile)], [ra])
                        if s == 0:
                            CP("act", xaP[:, n, tile * 2:tile * 2 + 2, 2:258], pa[:].rearrange("p (a b) -> p a b", a=2), [ra], [Bxa.r(n, tile)])
                        else:
                            CP("act", xaS[:, n, 0, 2 + tile * 512:2 + (tile + 1) * 512], pa[:], [ra], [Bxa.r(n, tile)])
            stage(f'L{l}s{s} T2 exch/LRU')
            if s == 1:
                Bbo_ = sample_exchange_xa_issue(l)
            Bd = kv_pass(l, s)
            if s == 1:
                Bao = sample_exchange_kv(l, Bd)
                sample_exchange_xa_finish(l, Bbo_)
                lru(l, s)
                Bco_ = lru_exchange(l)
                QS[0] = 1
                q_proj(l, s, C_QB, qkgc[:, l * 2:l * 2 + 1], True)
                QS[0] = 0
                lru_exchange_finish(l, Bco_)
                lru_fix(l)
            else:
                lru(l, s)
            stage(f'L{l}s{s} T4')
            branch_proj(l, 0, w_oa, 8, C_GA, list(range(8)))
            stage(f'L{l}s{s} T5')
            if s == 0:
                q_proj(l, s, C_QB, qkgc[:, l * 2:l * 2 + 1], False)
            if s == 0:
                load_kv_local(Bd[0], kvd[l][0][0], 0, 1024, 0, 0)
                attn_P(l, "b", 0)
            else:
                load_ctx(l, ckb, cvb)
                DMA("sp", KT[:, 256:4352].rearrange("p (r n) -> p r n", r=4), ccA_out[l][0][:, 0:1024].rearrange("(r p) n -> p r n", p=128),
                    [Bao[0].r()], [BKT.r("loc")])
                DMA("sp", VV[:, 2:34, :].rearrange("p (r b) c -> p r (b c)", r=4),
                    ccA_out[l][0][:, 1024:2560].rearrange("(r p) n -> p r n", p=128), [Bao[0].r()], [BVV.r("loc")])
                attn_SB(l)
            for j in range(4):
                for tile in range(2):
                    subs = [r_ for k_, r_ in Bo._res.items() if len(k_) == 5 and k_[0] == j and k_[1] == tile]
                    S.add("dve", lambda e: e.engine_nop(), reads=subs, writes=[Bo.r(j, tile)])
            branch_proj(l, 1, w_ob, 4, C_GB, [0, 1, 2, 3])
            stage(f'L{l}s{s} T7')
            q_proj(l, s, C_QC, None, s == 1)
            if s == 0:
                load_kv_local(Bd[1], kvd[l][0][1], 0, 1024, 0, 0)
                attn_P(l, "c", 0)
            else:
                load_ctx(l, ckc, cvc)
                DMA("sp", KT[:, 384:1408], kvd[l][1][1][:, 0:1024], [Bd[1].r()], [BKT.r("loc")])
                DMA("sp", VV[:, 3:11, :], kvd[l][1][1][:, 1024:2560].rearrange("p (b c) -> p b c", c=192), [Bd[1].r()], [BVV.r("loc")])
                for e_, blk in ((0, 7), (1, 0)):
                    DMA("sp", candK[:, e_, :, :], ccA_out[l][1][:, blk * 128:(blk + 1) * 128].rearrange("(r p) n -> p r n", p=128),
                        [Bao[1].r()], [BcK.r(e_)])
                    DMA("sp", candV[:, e_, :, :], ccA_out[l][1][:, 1024 + blk * 192:1024 + (blk + 1) * 192].rearrange("(r p) n -> p r n", p=128),
                        [Bao[1].r()], [BcV.r(e_)])
                for e_, so, kdst, vblk in ((0, 0, 256, 2), (1, 4, 1408, 11)):
                    TS("dve", KT[:, kdst:kdst + 128], candK[:, e_, 0, :], sel[:, so:so + 1], ALU.mult, [BcK.r(e_), Bsel.r()], [BKT.r("halo")])
                    TS("dve", VV[:, vblk, :], candV[:, e_, 0, :], sel[:, so:so + 1], ALU.mult, [BcV.r(e_), Bsel.r()], [BVV.r("halo")])
                    for r_ in range(1, 4):
                        STT("dve", KT[:, kdst:kdst + 128], candK[:, e_, r_, :], sel[:, so + r_:so + r_ + 1], KT[:, kdst:kdst + 128],
                            ALU.mult, ALU.add, [BcK.r(e_), Bsel.r(), BKT.r("halo")], [BKT.r("halo")])
                        STT("dve", VV[:, vblk, :], candV[:, e_, r_, :], sel[:, so + r_:so + r_ + 1], VV[:, vblk, :],
                            ALU.mult, ALU.add, [BcV.r(e_), Bsel.r(), BVV.r("halo")], [BVV.r("halo")])
                attn_SC(l, 0)
            for j in range(0, 4):
                for tile in range(2):
                    subs = [r_ for k_, r_ in Bo._res.items() if len(k_) == 5 and k_[0] == j and k_[1] == tile]
                    S.add("dve", lambda e: e.engine_nop(), reads=subs, writes=[Bo.r(j, tile)])
            branch_proj(l, 2, w_oc, 4, C_GC, [0, 1, 2, 3])
            stage(f'L{l}s{s} T9')
            out_proj_residual(l, s, w_out, 8, mergedb, lambda kc, tile: Bmgb.r(kc, tile), lambda oc: AA[:, l, s, 2, oc:oc + 1])
            stage(f'L{l}s{s} FFN')
            rmsnorm_h(s, lambda kc: AA[:, l, s, 1, kc:kc + 1], lambda kc: mod[:, l, 24 + kc, cidx:cidx + 1])
            ffn(l, s)
            out_proj_residual(l, s, w_down, 22, actb, lambda kc, tile: Bact.r(kc, tile), lambda oc: mod[:, l, 40 + oc, cidx:cidx + 1])
            if s == 0:
                pa, ra = ps_mm()
                TR(pa[0:64, 0:128], lst[:].rearrange("p a b c -> p (a b c)"), Blst.rs(range(4), range(2), range(8)), [ra])
                CP("dve", lstT[:], pa[0:64, 0:128], [ra], [BlstT.r()])
                DMA("sp", nlru[l], lstT[:], [BlstT.r()], [])

    stage('final')
    yb, Byb = region("yb", XO, [128, 8, 512], F32)
    yt = [region(f"yt{i}", XO + 16384 + i * 4096, [128, 1024], F32) for i in range(2)]
    fgc = lambda kc: PT[:, PRM["fg"] + kc:PRM["fg"] + kc + 1]
    cnt = 0
    for s in range(2):
        for tile in range(2):
            tok = slice(s * T + tile * 512, s * T + (tile + 1) * 512)
            xr = Bx.rs([s], range(8), [tile])
            ACT(nsq[:], xres[:, :, tok], AF.Square, xr, [Bnsq.r()])
            pa, ra = ps_mm()
            for kc in range(8):
                MM(pa[:], onesb[:], nsq[:, kc, :], kc == 0, kc == 7, [Bones.r(), Bnsq.r()], [ra])
            ACT(nrs[:], pa[:], AF.Sqrt, [ra, Beps.r()], [Bnrs.r()], bias=epsc[:], scale=1.0 / 1024)
            S.add("dve", lambda e: e.reciprocal(out=nrs[:], in_=nrs[:]), reads=[Bnrs.r()], writes=[Bnrs.r()])
            for kc in range(8):
                STT("dve", yb[:, kc, :], xres[:, kc, tok], fgc(kc), nrs[:], ALU.mult, ALU.mult, [Bx.r(s, kc, tile), Bnrs.r()] + RPT, [Byb.r(kc)])
            for t4 in range(4):
                yo, Byo = yt[cnt % 2]
                cnt += 1
                for g in range(2):
                    pa, ra = ps_mm()
                    for j in range(4):
                        kc = g * 4 + j
                        TR(pa[:, j * 128:(j + 1) * 128], yb[:, kc, t4 * 128:(t4 + 1) * 128], [Byb.r(kc)], [ra])
                    CP("act" if g else "dve", yo[:, g * 512:(g + 1) * 512], pa[:], [ra], [Byo.r(g)])
                row = s * T + tile * 512 + t4 * 128
                DMA("sp", y_out[row:row + 128, :], yo[:], [Byo.r(0), Byo.r(1)], [])

    with nc.allow_non_contiguous_dma(reason="small strided param / edge transfers"):
        S.emit()
    nc._sched = S
    return nc


def _rope_tables(core):
    r = core % 4
    t = np.arange(1024) + r * 1024
    row = (t // 64).astype(np.float32)
    col = (t % 64).astype(np.float32)
    d = np.arange(128) % 64
    half = d // 32
    jj = (d % 32) % 16
    inv = (10000.0 ** (-(jj.astype(np.float32)) / 16.0)).astype(np.float32)
    pos = np.where(half[:, None] == 0, row[None, :], col[None, :]).astype(np.float32)
    ang = (pos * inv[:, None]).astype(np.float32)
    return np.cos(ang).astype(np.float32), np.sin(ang).astype(np.float32)


def _rot_matrix():
    R = np.zeros((128, 128), np.float32)
    for m in range(128):
        if m % 32 < 16:
            R[m + 16, m] = -1.0
        else:
            R[m - 16, m] = 1.0
    return R


_NC_CACHE = {}
NCORES = 8
RAW = {}


def kernel(x_prompt, x_sample, c, cache_kb, cache_vb, cache_kc, cache_vc, state_lru,
           c_ctx, norm1_g, norm2_g, w_mod, b_mod, w_in, lru_conv_w, lru_conv_b,
           lru_wa, lru_ba, lru_wx, lru_bx, lru_lam, qnorm_g, knorm_g, sink_c,
           w_oa, w_ob, w_oc, w_out, w_up, ffn_conv_w, ffn_conv_b, w_down, final_g):
    f = lambda a: np.ascontiguousarray(np.asarray(a, dtype=np.float32))
    x_prompt, x_sample = f(x_prompt), f(x_sample)
    if "nc" not in _NC_CACHE:
        _NC_CACHE["nc"] = build()
    nc = _NC_CACHE["nc"]
    shared = {
        "w_mod": f(w_mod), "w_in": f(w_in), "lru_wa": f(lru_wa), "lru_wx": f(lru_wx), "w_oa": f(w_oa), "w_ob": f(w_ob),
        "w_oc": f(w_oc), "w_out": f(w_out), "w_up": f(w_up), "w_down": f(w_down),
        "qkg": np.ascontiguousarray(np.stack([f(qnorm_g), f(knorm_g)], axis=1)),
        "sinkc": f(sink_c), "identd": np.eye(128, dtype=np.float32), "rotd": _rot_matrix(),
    }
    jj, ii = np.meshgrid(np.arange(128), np.arange(128), indexing="ij")
    mprev = (jj >= ii).astype(np.float32)
    mnext = (jj <= ii).astype(np.float32)
    in_maps = []
    for core in range(8):
        g, r = core // 4, core % 4
        xin = np.concatenate([x_prompt[4 * core:4 * core + 4].reshape(1024, 1024), x_sample[g, r * 1024:(r + 1) * 1024]], axis=0)
        rows = [f(c_ctx).reshape(8, 128), f(c)[g].reshape(8, 128), f(norm1_g).reshape(16, 128), f(norm2_g).reshape(16, 128),
                f(b_mod).reshape(96, 128), f(lru_conv_w).reshape(64, 128), f(lru_conv_b).reshape(16, 128),
                f(lru_ba).reshape(32, 128), f(lru_bx).reshape(32, 128), f(lru_lam).reshape(32, 128),
                f(ffn_conv_w).reshape(132, 128), f(ffn_conv_b).reshape(44, 128), f(final_g).reshape(8, 128),
                f(state_lru)[g].reshape(32, 128)]
        prm = np.concatenate(rows, axis=0)
        prm = np.concatenate([prm, np.zeros((PRM_ROWS - prm.shape[0], 128), np.float32)], axis=0)
        cosd, sind = _rope_tables(core)
        selv = np.zeros((128, 16), np.float32)
        if r > 0:
            selv[:, r - 1] = 1.0
        if r < 3:
            selv[:, 4 + r + 1] = 1.0
        selv[:, 8 + r] = 1.0
        maskd = np.stack([mprev, mnext, mprev * (1.0 if r > 0 else 0.0), mnext * (1.0 if r < 3 else 0.0)], axis=0).astype(np.float32)
        m = dict(shared)
        m.update({
            "xin": np.ascontiguousarray(xin), "prm": np.ascontiguousarray(prm),
            "ckb": np.ascontiguousarray(f(cache_kb)[g].reshape(2, 256, 128)), "cvb": np.ascontiguousarray(f(cache_vb)[g].reshape(2, 256, 128)),
            "ckc": np.ascontiguousarray(f(cache_kc)[g].reshape(2, 256, 128)), "cvc": np.ascontiguousarray(f(cache_vc)[g].reshape(2, 256, 128)),
            "cosd": cosd, "sind": sind, "maskd": maskd, "seld": selv,
        })
        in_maps.append(m)
    res = run_bass_kernel_spmd(nc, in_maps[:NCORES], core_ids=list(range(NCORES)))
    R = res.results
    if NCORES != 8:
        RAW['R'] = R
        return None
    y_prompt = np.stack([R[cidx]["y_out"][0:1024].reshape(4, 256, 1024) for cidx in range(8)], 0).reshape(32, 256, 1024)
    y_sample = np.stack([np.concatenate([R[g * 4 + r]["y_out"][1024:2048] for r in range(4)], 0) for g in range(2)], 0)
    nkv = np.stack([R[cidx]["nkv"] for cidx in range(8)], 0)
    nkv = nkv.reshape(8, 2, 4, 256, 4, 2, 64).transpose(0, 2, 1, 3, 4, 5, 6).reshape(32, 2, 256, 4, 2, 64)
    new_kb, new_vb, new_kc, new_vc = (np.ascontiguousarray(nkv[:, :, :, i]) for i in range(4))
    nl = np.stack([R[cidx]["nlru"] for cidx in range(8)], 0)
    new_lru = nl.reshape(8, 2, 4, 2, 8, 128).transpose(0, 2, 1, 3, 4, 5).reshape(32, 2, 2, 1024)
    return (y_prompt.astype(np.float32), y_sample.astype(np.float32), new_kb, new_vb, new_kc, new_vc,
            np.ascontiguousarray(new_lru))
```
